# Optimizing a Trainium2 kernel written in Bass

```python
import jax, jax.numpy as jnp
from jax import lax
import numpy as np

D_MODEL = 1024
BATCH = 4
SEQ = 4096
DEPTH = 4

CONV_WIDTH = 31
RET_HEADS = 4
RET_QK_DIM = D_MODEL
RET_V_DIM = 2 * D_MODEL
RET_HEAD_QK = RET_QK_DIM // RET_HEADS
RET_HEAD_V = RET_V_DIM // RET_HEADS
RET_CHUNK = 128
ROPE_BASE = 10000.0
FFN_DIM = 2816
FFN_CONV_WIDTH = 3
N_MIXERS = 2
N_CONV_LAYERS = (DEPTH + 1) // 2
N_RET_LAYERS = DEPTH // 2
EPS = 1e-6

kernel_name = "hybrid_conformer_conv_retnet_convffn"


def rmsnorm(x, g):
    xf = x.astype(jnp.float32)
    y = xf * lax.rsqrt(jnp.mean(xf * xf, axis=-1, keepdims=True) + EPS)
    return y.astype(x.dtype) * g


def causal_dwconv(x, w, b):
    k = w.shape[0]
    y = lax.conv_general_dilated(
        x, w[:, None, :].astype(x.dtype), window_strides=(1,), padding=[(k - 1, 0)],
        dimension_numbers=("NWC", "WIO", "NWC"), feature_group_count=x.shape[-1])
    return y + b


def conv_module(x, w_in, dw_w, dw_b, ln_g, ln_b, w_out):
    h = x @ w_in
    a, gate = jnp.split(h, 2, axis=-1)
    h = a * jax.nn.sigmoid(gate)
    h = causal_dwconv(h, dw_w, dw_b)
    hf = h.astype(jnp.float32)
    mu = jnp.mean(hf, axis=-1, keepdims=True)
    var = jnp.mean(jnp.square(hf - mu), axis=-1, keepdims=True)
    h = ((hf - mu) * lax.rsqrt(var + EPS)).astype(x.dtype) * ln_g + ln_b
    h = jax.nn.silu(h)
    return h @ w_out


def rotary(t, positions):
    d = t.shape[-1]
    inv_freq = 1.0 / (ROPE_BASE ** (jnp.arange(0, d, 2, dtype=jnp.float32) / d))
    ang = positions.astype(jnp.float32)[..., None] * inv_freq
    cos = jnp.cos(ang)[:, :, None, :]
    sin = jnp.sin(ang)[:, :, None, :]
    tf = t.astype(jnp.float32)
    t1, t2 = tf[..., : d // 2], tf[..., d // 2:]
    return jnp.concatenate([t1 * cos - t2 * sin, t2 * cos + t1 * sin], axis=-1).astype(t.dtype)


def retention(x, positions, w_in, gn_g, w_out):
    b, s, _ = x.shape
    nc = s // RET_CHUNK
    h = x @ w_in
    q, k, v, g = jnp.split(h, [RET_QK_DIM, 2 * RET_QK_DIM, 2 * RET_QK_DIM + RET_V_DIM], axis=-1)
    q = rotary(q.reshape(b, s, RET_HEADS, RET_HEAD_QK), positions)
    k = rotary(k.reshape(b, s, RET_HEADS, RET_HEAD_QK), positions) * (RET_HEAD_QK ** -0.5)
    v = v.reshape(b, s, RET_HEADS, RET_HEAD_V)

    def to_chunks(t):
        return t.reshape(b, nc, RET_CHUNK, RET_HEADS, t.shape[-1]).transpose(1, 0, 3, 2, 4)

    qc, kc, vc = to_chunks(q), to_chunks(k), to_chunks(v)
    dt = q.dtype
    log_gamma = jnp.log1p(-jnp.exp2(-5.0 - jnp.arange(RET_HEADS, dtype=jnp.float32)))
    idx = jnp.arange(RET_CHUNK, dtype=jnp.float32)
    dist = idx[:, None] - idx[None, :]
    intra = jnp.where(dist[None] >= 0,
                      jnp.exp(log_gamma[:, None, None] * jnp.maximum(dist, 0.0)[None]), 0.0).astype(dt)
    q_decay = jnp.exp(log_gamma[:, None] * (idx + 1.0)[None]).astype(dt)[:, :, None]
    k_decay = jnp.exp(log_gamma[:, None] * (RET_CHUNK - 1.0 - idx)[None]).astype(dt)[:, :, None]
    chunk_decay = jnp.exp(log_gamma * RET_CHUNK).astype(dt)[:, None, None]

    def step(state, qkv):
        qi, ki, vi = qkv
        scores = jnp.einsum("bhnd,bhmd->bhnm", qi, ki) * intra
        inner = jnp.einsum("bhnm,bhme->bhne", scores, vi)
        cross = jnp.einsum("bhnd,bhde->bhne", qi, state) * q_decay
        new_state = chunk_decay * state + jnp.einsum("bhmd,bhme->bhde", ki * k_decay, vi)
        return new_state, inner + cross

    state0 = jnp.zeros((b, RET_HEADS, RET_HEAD_QK, RET_HEAD_V), dtype=dt)
    _, o = lax.scan(step, state0, (qc, kc, vc))
    o = o.transpose(1, 0, 3, 2, 4).reshape(b, s, RET_HEADS, RET_HEAD_V)
    of = o.astype(jnp.float32)
    mu = jnp.mean(of, axis=-1, keepdims=True)
    var = jnp.mean(jnp.square(of - mu), axis=-1, keepdims=True)
    o = ((of - mu) * lax.rsqrt(var + EPS)).astype(x.dtype).reshape(b, s, RET_V_DIM) * gn_g
    return (jax.nn.silu(g) * o) @ w_out


def conv_ffn(x, w_in, dw_w, dw_b, w_out):
    h = x @ w_in
    a, u = jnp.split(h, 2, axis=-1)
    a = causal_dwconv(a, dw_w, dw_b)
    return (jax.nn.silu(a) * u) @ w_out


def setup_inputs(seed: int = 0) -> dict:
    key = jax.random.key(seed)
    ks = jax.random.split(key, 20)
    D, F = D_MODEL, FFN_DIM
    na, nb = N_CONV_LAYERS, N_RET_LAYERS

    def nrm(k, shape, scale):
        return jax.random.normal(k, shape, dtype=jnp.float32) * scale

    ret_in = 2 * RET_QK_DIM + 2 * RET_V_DIM
    return {
        "x": nrm(ks[0], (BATCH, SEQ, D), 1.0),
        "positions": jnp.broadcast_to(jnp.arange(SEQ, dtype=jnp.int32)[None, :], (BATCH, SEQ)),
        "conv_w_in": nrm(ks[1], (na, D, 2 * D), D ** -0.5),
        "conv_dw_w": nrm(ks[2], (na, CONV_WIDTH, D), CONV_WIDTH ** -0.5),
        "conv_dw_b": nrm(ks[3], (na, D), 0.01),
        "conv_ln_g": 1.0 + nrm(ks[4], (na, D), 0.01),
        "conv_ln_b": nrm(ks[5], (na, D), 0.01),
        "conv_w_out": nrm(ks[6], (na, D, D), D ** -0.5),
        "ret_w_in": nrm(ks[7], (nb, D, ret_in), D ** -0.5),
        "ret_gn_g": 1.0 + nrm(ks[8], (nb, RET_V_DIM), 0.01),
        "ret_w_out": nrm(ks[9], (nb, RET_V_DIM, D), RET_V_DIM ** -0.5),
        "ffn_w_in": nrm(ks[10], (DEPTH, D, 2 * F), D ** -0.5),
        "ffn_dw_w": nrm(ks[11], (DEPTH, FFN_CONV_WIDTH, F), FFN_CONV_WIDTH ** -0.5),
        "ffn_dw_b": nrm(ks[12], (DEPTH, F), 0.01),
        "ffn_w_out": nrm(ks[13], (DEPTH, F, D), F ** -0.5),
        "norm_mix_g": 1.0 + nrm(ks[14], (DEPTH, D), 0.01),
        "norm_ffn_g": 1.0 + nrm(ks[15], (DEPTH, D), 0.01),
        "final_g": 1.0 + nrm(ks[16], (D,), 0.01),
    }


def reference(x, positions, conv_w_in, conv_dw_w, conv_dw_b, conv_ln_g, conv_ln_b, conv_w_out,
              ret_w_in, ret_gn_g, ret_w_out, ffn_w_in, ffn_dw_w, ffn_dw_b, ffn_w_out,
              norm_mix_g, norm_ffn_g, final_g):
    for i in range(DEPTH):
        h = rmsnorm(x, norm_mix_g[i])
        j = i // N_MIXERS
        if i % N_MIXERS == 0:
            x = x + conv_module(h, conv_w_in[j], conv_dw_w[j], conv_dw_b[j],
                                conv_ln_g[j], conv_ln_b[j], conv_w_out[j])
        else:
            x = x + retention(h, positions, ret_w_in[j], ret_gn_g[j], ret_w_out[j])
        h = rmsnorm(x, norm_ffn_g[i])
        x = x + conv_ffn(h, ffn_w_in[i], ffn_dw_w[i], ffn_dw_b[i], ffn_w_out[i])
    return rmsnorm(x, final_g)
```

```python
import contextlib
import math
import numpy as np
import concourse.bass as bass
import concourse.mybir as mybir
from concourse.bass_utils import run_bass_kernel_spmd

F32 = mybir.dt.float32
BF16 = mybir.dt.bfloat16
I32 = mybir.dt.int32
AF = mybir.ActivationFunctionType
ALU = mybir.AluOpType

P = 128
D = 1024
KC = 8
FF = 2816
NFC = 22
CW = 31
HALO = 32
NH = 4
DK = 256
DV = 512
CH = 128
EPS = 1e-6
TW = 512
NSLOT = 5
EPOCH = 12000
PAIRS = [[0, 1], [2, 3], [4, 5], [6, 7]]


class Eng:
    def __init__(self, sched, name, handle, unit=1, is_pe=False, is_dma=False):
        self.s = sched
        self.name = name
        self.h = handle
        self.unit = unit
        self.is_pe = is_pe
        self.is_dma = is_dma
        self.count = 0
        self.sems = []
        self.waited = {}
        self.max_waited = 0

    def sem_val(self, idx):
        e = (idx - 1) // EPOCH
        while len(self.sems) <= e:
            self.sems.append(self.s.new_sem(f"{self.name}_{len(self.sems)}"))
        return self.sems[e], (idx - e * EPOCH) * self.unit


class Cell:
    __slots__ = ("w", "r")

    def __init__(self):
        self.w = None
        self.r = {}


class Sched:
    def __init__(self, nc, stack):
        self.nc = nc
        self.stack = stack
        self.cells = {}
        self.nsem = 0
        self.pe = Eng(self, "pe", nc.tensor, is_pe=True)
        self.act = Eng(self, "act", nc.scalar)
        self.dve = Eng(self, "dve", nc.vector)
        self.pool = Eng(self, "pool", nc.gpsimd)
        self.sp = Eng(self, "sp", nc.sync)
        self.engines = [self.pe, self.act, self.dve, self.pool, self.sp]
        self.streams = []

    def new_sem(self, name):
        self.nsem += 1
        return self.stack.enter_context(self.nc.semaphore(name))

    def stream(self, name, unit=16):
        st = Eng(self, name, None, unit=unit, is_dma=True)
        self.streams.append(st)
        return st

    def cell(self, k):
        c = self.cells.get(k)
        if c is None:
            c = self.cells[k] = Cell()
        return c

    def _deps(self, R, W):
        deps = {}

        def add(e, idx):
            if deps.get(e, 0) < idx:
                deps[e] = idx
        for k in R:
            c = self.cell(k)
            if c.w is not None:
                add(*c.w)
        for k in W:
            c = self.cell(k)
            if c.w is not None:
                add(*c.w)
            for e, idx in c.r.values():
                add(e, idx)
        return deps

    def _wait(self, eng, deps):
        for e, idx in deps.items():
            if e is eng and eng.is_pe:
                continue
            if e.is_dma:
                idx = e.count
            if eng.waited.get(e.name, 0) >= idx:
                continue
            sem, val = e.sem_val(idx)
            eng.h.wait_ge(sem, val)
            eng.waited[e.name] = idx
            if e.is_dma and e.max_waited < idx:
                e.max_waited = idx

    def _wait_exact(self, eng, deps, own_stream):
        rest = {e: i for e, i in deps.items() if e is not own_stream}
        self._wait(eng, rest)
        if own_stream in deps:
            idx = deps[own_stream]
            if eng.waited.get(own_stream.name, 0) < idx:
                sem, val = own_stream.sem_val(idx)
                eng.h.wait_ge(sem, val)
                eng.waited[own_stream.name] = idx

    def _record(self, who, idx, R, W):
        for k in R:
            self.cell(k).r[who.name] = (who, idx)
        for k in W:
            c = self.cell(k)
            c.w = (who, idx)
            c.r = {}

    def op(self, eng, fn, R=(), W=(), inc=True):
        self._wait(eng, self._deps(R, W))
        inst = fn()
        if inc:
            eng.count += 1
            sem, val = eng.sem_val(eng.count)
            inst.then_inc(sem, 1)
            idx = eng.count
        else:
            idx = eng.count + 1
        self._record(eng, idx, R, W)
        return inst

    def dma(self, issuer, stream, out, in_, R=(), W=(), **kw):
        deps = self._deps(R, W)
        if stream.max_waited > 0:
            deps[stream] = max(deps.get(stream, 0), stream.max_waited)
        self._wait_exact(issuer, deps, stream)
        stream.count += 1
        sem, val = stream.sem_val(stream.count)
        issuer.h.dma_start(out=out, in_=in_, **kw).then_inc(sem, 16)
        self._record(stream, stream.count, R, W)

    def collective(self, stream, ins, outs, R=(), W=()):
        issuer = self.pool
        deps = self._deps(R, W)
        if stream.max_waited > 0:
            deps[stream] = max(deps.get(stream, 0), stream.max_waited)
        self._wait_exact(issuer, deps, stream)
        stream.count += 1
        sem, val = stream.sem_val(stream.count)
        issuer.h.collective_compute("AllGather", ALU.bypass, replica_groups=PAIRS,
                                    ins=ins, outs=outs).then_inc(sem, 1)
        self._record(stream, stream.count, R, W)

    def barrier(self):
        for eng in (self.pe, self.act, self.dve, self.pool, self.sp):
            deps = {}
            for e in self.engines + self.streams:
                if e.count > 0 and e is not eng and not e.name.startswith("w"):
                    deps[e] = e.count
            saved = eng.is_pe
            self._wait(eng, deps)

    def finish(self):
        for eng in (self.sp,):
            deps = {e: e.count for e in self.streams + self.engines if e.count > 0 and e is not eng}
            self._wait(eng, deps)


class WStream:
    def __init__(self, S, slots):
        self.S = S
        self.slots = slots
        self.plan = []
        self.issued = 0
        self.next = 0
        self.streams = [S.stream(f"w{i}") for i in range(len(slots))]

    def add(self, loads):
        self.plan.append(loads)

    def _issue_upto(self, limit):
        S = self.S
        while self.issued < min(limit, len(self.plan)):
            i = self.issued
            s = i % len(self.slots)
            for dst_fn, src in self.plan[i]:
                S.dma(S.pool, self.streams[s], dst_fn(self.slots[s]), src, R=(), W=[("W", s)])
            self.issued += 1

    def acquire(self, n):
        c = self.next
        self._issue_upto(c + len(self.slots))
        assert self.issued >= c + n, (self.issued, c, n)
        res = [((c + j) % len(self.slots)) for j in range(n)]
        self.next += n
        return res


def split_groups(n, g):
    out = []
    i = 0
    while i < n:
        m = min(g, n - i)
        out.append(list(range(i, i + m)))
        i += m
    return out


class Cfg:
    def __init__(self, T, layers, final_norm=True):
        self.T = T
        self.NT = T // TW
        self.NCH = T // CH
        self.layers = layers
        self.final_norm = final_norm


def vec_layout():
    off = {}
    n = 0

    def add(name, cnt):
        nonlocal n
        off[name] = n
        n += cnt
    for i in range(4):
        add(("nmix", i), KC)
        add(("nffn", i), KC)
        add(("fdw", i), 3 * NFC)
        add(("fdb", i), NFC)
    add("nfin", KC)
    for j in range(2):
        add(("cdw", j), CW * KC)
        add(("cdb", j), KC)
        add(("clg", j), KC)
        add(("clb", j), KC)
        add(("gng", j), 16)
    add("flag", 1)
    add("invf", 1)
    for h in range(NH):
        add(("kdec", h), 1)
        add(("gd", h), 16)
        add(("qc", h), 1)
        add(("qc2", h), 1)
        add(("epsn", h), 1)
    return off, n


def build_program(cfg):
    T, NT, NCH = cfg.T, cfg.NT, cfg.NCH
    TX = HALO + T
    nc = bass.Bass("TRN2", target_bir_lowering=False)
    voff, NV = vec_layout()

    def din(name, shape, dt=F32):
        return nc.dram_tensor(name, shape, dt, kind="ExternalInput").ap()

    xT = din("xT", [D, TX])
    pos = din("pos", [1, T], I32)
    vecs_d = din("vecs", [P, NV])
    consts_d = din("cmask", [P, NH * 2 * CH])
    ident_d = din("ident", [P, P])
    conv_w_in = din("conv_w_in", [2, D, 2 * D])
    conv_w_out = din("conv_w_out", [2, D, D])
    ret_w_in = din("ret_w_in", [2, D, 6144])
    ret_w_out = din("ret_w_out", [2, 2048, D])
    ffn_w_in = din("ffn_w_in", [4, D, 2 * FF])
    ffn_w_out = din("ffn_w_out", [4, FF, D])
    outT = nc.dram_tensor("outT", [D, T], F32, kind="ExternalOutput").ap()

    n_x_exch = 8
    msgX = [nc.dram_tensor(f"msgX{i}", [P, KC * HALO], F32, kind="Internal").ap() for i in range(n_x_exch)]
    gathX = [nc.dram_tensor(f"gathX{i}", [2 * P, KC * HALO], F32, kind="Internal").ap() for i in range(n_x_exch)]
    msgS = [[nc.dram_tensor(f"msgS{i}_{h}", [P, 2 * DV], F32, kind="Internal").ap() for h in range(NH)] for i in range(2)]
    gathS = [[nc.dram_tensor(f"gathS{i}_{h}", [2 * P, 2 * DV], F32, kind="Internal").ap() for h in range(NH)] for i in range(2)]

    kvs = [[[nc.dram_tensor(f"kvs{jj}_{h}_{t}", [P, 4096], BF16, kind="Internal").ap() for t in range(NT)]
            for h in range(NH)] for jj in range(2)]

    stack = contextlib.ExitStack()
    with stack:
        S = Sched(nc, stack)
        PE, ACT, DVE, POOL, SP = S.pe, S.act, S.dve, S.pool, S.sp

        def sb(name, shape, dt):
            return stack.enter_context(nc.sbuf_tensor(name, shape, dt))

        uniq = [0]

        def uname(name):
            uniq[0] += 1
            return f"{name}_u{uniq[0]}"

        X = sb("X", [P, KC, TX], F32)
        Hn = sb("Hn", [P, KC, TX], BF16)
        slots = [sb(f"wslot{i}", [P, KC, TW], BF16) for i in range(NSLOT)]
        VEC = sb("VEC", [P, NV], F32)
        IDENT = sb("IDENT", [P, P], BF16)
        ONES = sb("ONES", [P, P], BF16)
        EPSV = sb("EPSV", [P, 1], F32)
        ps_lo = [stack.enter_context(nc.psum_tensor(f"ps{i}", [P, TW], F32)) for i in range(2)]
        psB = stack.enter_context(nc.psum_tensor("psB", [P, 2, TW], F32))
        ps_hi = [stack.enter_context(nc.psum_tensor(f"ps{i}", [P, TW], F32)) for i in range(4, 7)]
        psum = [ps_lo[0][:, :], ps_lo[1][:, :], psB[:, 0, :], psB[:, 1, :]] + [t_[:, :] for t_ in ps_hi]
        psb = stack.enter_context(nc.psum_tensor("psb", [P, 2 * TW], BF16))
        ps_rr = {"A": [0, 1], "B": [2, 3], "C": [4, 5], "D": [6]}
        ps_ctr = {k: 0 for k in ps_rr}

        def bank(pool):
            lst = ps_rr[pool]
            b = lst[ps_ctr[pool] % len(lst)]
            ps_ctr[pool] += 1
            return b

        ws = WStream(S, slots)
        ld = S.stream("ld")
        ldp = S.stream("ldp")
        st_out = S.stream("st")
        xch = S.stream("xch")
        kvst = S.stream("kvst")
        kvld = S.stream("kvld")
        cc = S.stream("cc", unit=1)

        def xc(t, lo=0, hi=None):
            if t < 0:
                return slice(0, HALO)
            base = HALO + t * TW
            return slice(base + lo, base + (TW if hi is None else hi))

        def vcol(name, i=0):
            o = voff[name] + i
            return VEC[:, o:o + 1]

        def wsrc(w2d, r0, nr, c0, ncol):
            return w2d[r0 * P:(r0 + nr) * P, c0:c0 + ncol].rearrange("(r p) n -> p r n", p=P)

        ffn_groups = split_groups(NFC, 4)

        def plan_conv(j):
            w_in, w_out = conv_w_in[j], conv_w_out[j]
            for half in range(2):
                ws.add([(lambda s: s[:, :, :], wsrc(w_in, 0, KC, half * 512, 512))])
                ws.add([(lambda s: s[:, :, :], wsrc(w_in, 0, KC, D + half * 512, 512))])
            for half in range(2):
                ws.add([(lambda s: s[:, :, :], wsrc(w_out, 0, KC, half * 512, 512))])

        def plan_ffn(i):
            w_in, w_out = ffn_w_in[i], ffn_w_out[i]
            for g in ffn_groups:
                n = len(g) * P
                ws.add([(lambda s, n=n: s[:, :, 0:n], wsrc(w_in, 0, KC, g[0] * P, n))])
                ws.add([(lambda s, n=n: s[:, :, 0:n], wsrc(w_in, 0, KC, FF + g[0] * P, n))])
                ws.add([(lambda s, g=g: s[:].rearrange("p a b -> p (a b)")[:, 0:len(g) * D]
                         .rearrange("p (r n) -> p r n", n=D),
                         wsrc(w_out, g[0], len(g), 0, D))])

        def plan_ret(j):
            w_in, w_out = ret_w_in[j], ret_w_out[j]

            for h in range(NH):
                ws.add([(lambda s: s[:, :, DK:2 * DK], wsrc(w_in, 0, KC, D + h * DK, DK))])
                ws.add([(lambda s: s[:, :, :], wsrc(w_in, 0, KC, 2 * D + h * DV, DV))])
            for h in range(NH):
                ws.add([(lambda s: s[:, :, 0:DK], wsrc(w_in, 0, KC, h * DK, DK))])
                ws.add([(lambda s: s[:, :, :], wsrc(w_in, 0, KC, 4 * D + h * DV, DV))])
                ws.add([(lambda s: s[:].rearrange("p a b -> p (a b)").rearrange("p (r n) -> p r n", n=D),
                         wsrc(w_out, h * 4, 4, 0, D))])

        for li in cfg.layers:
            if li % 2 == 0:
                plan_conv(li // 2)
            else:
                plan_ret(li // 2)
            plan_ffn(li)

        xT3 = xT.rearrange("(c p) n -> p c n", p=P)
        outT3 = outT.rearrange("(c p) n -> p c n", p=P)
        S.dma(SP, ld, VEC[:, :], vecs_d, W=["VEC"])
        for t in list(range(NT)) + [-1]:
            ldx = S.stream(f"ldx{t + 1}")
            S.dma(SP, ldx, X[:, :, xc(t)], xT3[:, :, xc(t)], W=[("X", c, t) for c in range(KC)])
        S.dma(POOL, ldp, IDENT[:, :], ident_d, W=["IDENT"])
        S.op(DVE, lambda: nc.vector.memset(ONES[:, :], 1.0), W=["ONES"])
        S.op(DVE, lambda: nc.vector.memset(EPSV[:, :], float(EPS)), W=["EPSV"])

        def mm(out, lhsT, rhs, start, stop, R, W, inc=None):
            S.op(PE, lambda: nc.tensor.matmul(out, lhsT, rhs, start=start, stop=stop), R=R, W=W,
                 inc=(stop if inc is None else inc))

        def rmsnorm(gname, tiles, scr):
            SQ, RS2 = scr
            for ti, t in enumerate(tiles):
                n = HALO if t < 0 else TW
                RS = RS2[ti % 2]
                rk = ("RS", ti % 2)
                b = bank("D") if ti % 2 == 0 else bank("C")
                for c in range(KC):
                    q = SQ[c % 2]
                    S.op(ACT, lambda: nc.scalar.activation(q[:, 0:n], X[:, c, xc(t)], AF.Square),
                         R=[("X", c, t)], W=[("SQ", c % 2)])
                    mm(psum[b][:, 0:n], ONES[:, :], q[:, 0:n], c == 0, c == KC - 1,
                       R=[("SQ", c % 2), "ONES"], W=[("ps", b)], inc=True)
                S.op(ACT, lambda: nc.scalar.activation(RS[:, 0:n], psum[b][:, 0:n], AF.Ln, bias=EPSV[:, 0:1],
                                                       scale=1.0 / D),
                     R=[("ps", b), "EPSV"], W=[rk])
                S.op(ACT, lambda: nc.scalar.activation(RS[:, 0:n], RS[:, 0:n], AF.Exp, scale=-0.5), R=[rk], W=[rk])
                for c in range(KC):
                    S.op(DVE, lambda: nc.vector.scalar_tensor_tensor(
                        Hn[:, c, xc(t)], X[:, c, xc(t)], vcol(gname, c), RS[:, 0:n],
                        op0=ALU.mult, op1=ALU.mult),
                        R=[("X", c, t), rk, "VEC"], W=[("Hn", c, t)])

        def residual_add(oc, t, b):
            S.op(DVE, lambda: nc.vector.tensor_tensor(X[:, oc, xc(t)], X[:, oc, xc(t)], psum[b][:, :], ALU.add),
                 R=[("ps", b), ("X", oc, t)], W=[("X", oc, t)])

        xcount = [0]

        def exchange_halo(XT):
            i = xcount[0]
            xcount[0] += 1
            tl = NT - 1
            for c in range(KC):
                S.op(ACT, lambda: nc.scalar.copy(XT[:, c, :], X[:, c, HALO + T - HALO:HALO + T]),
                     R=[("X", c, tl)], W=["XT"])
            S.dma(SP, xch, msgX[i], XT[:].rearrange("p c h -> p (c h)"), R=["XT"], W=[("msgX", i)])
            S.collective(cc, [msgX[i]], [gathX[i]], R=[("msgX", i)], W=[("gathX", i)])
            S.dma(SP, xch, XT[:].rearrange("p c h -> p (c h)"), gathX[i][0:P, :], R=[("gathX", i)], W=["XT"])
            for c in range(KC):
                S.op(DVE, lambda: nc.vector.tensor_scalar(X[:, c, 0:HALO], XT[:, c, :], vcol("flag"), None,
                                                          op0=ALU.mult),
                     R=["XT", "VEC"], W=[("X", c, -1)])

        def ffn(i):
            with contextlib.ExitStack() as es:
                def asb(name, shape, dt):
                    return es.enter_context(nc.sbuf_tensor(uname(name), shape, dt))
                SQ = [asb(f"f_sq{k}", [P, TW], BF16) for k in range(2)]
                RS = [asb(f"f_rs{k}", [P, TW], F32) for k in range(2)]
                AB = [asb(f"f_ab{k}", [P, TW + 2], F32) for k in range(2)]
                CB = [asb(f"f_cb{k}", [P, TW], F32) for k in range(2)]
                SL = [asb(f"f_sl{k}", [P, TW], F32) for k in range(2)]
                Z = [asb(f"f_z{k}", [P, 4, TW], BF16) for k in range(2)]
                TAIL = asb("f_tail", [P, NFC, 2], F32)
                rmsnorm(("nffn", i), list(range(0, NT)) + [-1], (SQ, RS))
                k_ab = 0
                for g in ffn_groups:
                    sa, su, so = ws.acquire(3)
                    A, U = slots[sa], slots[su]
                    O = slots[so][:].rearrange("p a b -> p (a b)")[:, 0:len(g) * D].rearrange("p (r n) -> p r n", n=D)
                    def out_proj(tt):
                        Zo = Z[tt % 2]
                        for oc in range(KC):
                            bo = bank("C")
                            for j in range(len(g)):
                                mm(psum[bo][:, :], O[:, j, oc * P:(oc + 1) * P], Zo[:, j, :],
                                   j == 0, j == len(g) - 1, R=[("W", so), ("Z", tt % 2, j)], W=[("ps", bo)])
                            residual_add(oc, tt, bo)

                    for t in range(NT):
                        Zt = Z[t % 2]
                        for j, fc in enumerate(g):
                            ba, bu = bank("A"), bank("B")
                            hn_r = [("Hn", kc, t) for kc in range(KC)]
                            for kc in range(KC):
                                mm(psum[ba][:, :], A[:, kc, j * P:(j + 1) * P], Hn[:, kc, xc(t)],
                                   kc == 0, kc == KC - 1, R=[("W", sa), ("Hn", kc, t)], W=[("ps", ba)])
                            for kc in range(KC):
                                mm(psum[bu][:, :], U[:, kc, j * P:(j + 1) * P], Hn[:, kc, xc(t)],
                                   kc == 0, kc == KC - 1, R=[("W", su), ("Hn", kc, t)], W=[("ps", bu)])
                            ab = AB[k_ab % 2]
                            cb = CB[k_ab % 2]
                            sl = SL[k_ab % 2]
                            kab = k_ab % 2
                            k_ab += 1
                            if t == 0:
                                bh = bank("D")
                                for kc in range(KC):
                                    mm(psum[bh][:, 0:2], A[:, kc, j * P:(j + 1) * P], Hn[:, kc, HALO - 2:HALO],
                                       kc == 0, kc == KC - 1, R=[("W", sa), ("Hn", kc, -1)], W=[("ps", bh)])
                                S.op(ACT, lambda: nc.scalar.copy(ab[:, 0:2], psum[bh][:, 0:2]),
                                     R=[("ps", bh)], W=[("AB", kab)])
                            else:
                                S.op(ACT, lambda: nc.scalar.copy(ab[:, 0:2], TAIL[:, fc, :]),
                                     R=[("TAIL", fc)], W=[("AB", kab)])
                            S.op(ACT, lambda: nc.scalar.copy(ab[:, 2:TW + 2], psum[ba][:, :]),
                                 R=[("ps", ba)], W=[("AB", kab)])
                            if t < NT - 1:
                                S.op(ACT, lambda: nc.scalar.copy(TAIL[:, fc, :], ab[:, TW:TW + 2]),
                                     R=[("AB", kab)], W=[("TAIL", fc)])
                            w0 = vcol(("fdw", i), 0 * NFC + fc)
                            w1 = vcol(("fdw", i), 1 * NFC + fc)
                            w2 = vcol(("fdw", i), 2 * NFC + fc)
                            bb = vcol(("fdb", i), fc)
                            S.op(DVE, lambda: nc.vector.tensor_scalar(cb[:, :], ab[:, 2:TW + 2], w2, bb,
                                                                      op0=ALU.mult, op1=ALU.add),
                                 R=[("AB", kab), "VEC"], W=[("CB", kab)])
                            S.op(DVE, lambda: nc.vector.scalar_tensor_tensor(cb[:, :], ab[:, 1:TW + 1], w1, cb[:, :],
                                                                             op0=ALU.mult, op1=ALU.add),
                                 R=[("AB", kab), ("CB", kab)], W=[("CB", kab)])
                            S.op(DVE, lambda: nc.vector.scalar_tensor_tensor(cb[:, :], ab[:, 0:TW], w0, cb[:, :],
                                                                             op0=ALU.mult, op1=ALU.add),
                                 R=[("AB", kab), ("CB", kab)], W=[("CB", kab)])
                            S.op(ACT, lambda: nc.scalar.activation(sl[:, :], cb[:, :], AF.Silu),
                                 R=[("CB", kab)], W=[("SL", kab)])
                            S.op(DVE, lambda: nc.vector.tensor_tensor(Zt[:, j, :], sl[:, :], psum[bu][:, :], ALU.mult),
                                 R=[("SL", kab), ("ps", bu)], W=[("Z", t % 2, j)])
                        if t >= 1:
                            out_proj(t - 1)
                    out_proj(NT - 1)
                S.barrier()

        def conv_module(j, li):
            with contextlib.ExitStack() as es:
                with contextlib.ExitStack() as es2:
                    def asb2(name, shape, dt):
                        return es2.enter_context(nc.sbuf_tensor(uname(name), shape, dt))
                    G = asb2("c_g", [P, KC, TX], BF16)
                    SQ = [asb2(f"c_sq{k}", [P, TW], BF16) for k in range(2)]
                    RS = [asb2(f"c_rs{k}", [P, TW], F32) for k in range(2)]
                    SG = [asb2(f"c_sg{k}", [P, TW], F32) for k in range(2)]
                    DG = [asb2(f"c_dg{k}", [P, CW, P], BF16) for k in range(2)]
                    rmsnorm(("nmix", li), list(range(0, NT)) + [-1], (SQ, RS))
                    ksg = 0
                    for half in range(2):
                        sa, sg = ws.acquire(2)
                        A, Gw = slots[sa], slots[sg]
                        for ccl in range(4):
                            cch = half * 4 + ccl
                            for t in list(range(0, NT)) + [-1]:
                                n = HALO if t < 0 else TW
                                ba, bg = bank("A"), bank("B")
                                for kc in range(KC):
                                    mm(psum[ba][:, 0:n], A[:, kc, ccl * P:(ccl + 1) * P], Hn[:, kc, xc(t)],
                                       kc == 0, kc == KC - 1, R=[("W", sa), ("Hn", kc, t)], W=[("ps", ba)])
                                for kc in range(KC):
                                    mm(psum[bg][:, 0:n], Gw[:, kc, ccl * P:(ccl + 1) * P], Hn[:, kc, xc(t)],
                                       kc == 0, kc == KC - 1, R=[("W", sg), ("Hn", kc, t)], W=[("ps", bg)])
                                sgt = SG[ksg % 2]
                                ks = ksg % 2
                                ksg += 1
                                S.op(ACT, lambda: nc.scalar.activation(sgt[:, 0:n], psum[bg][:, 0:n], AF.Sigmoid),
                                     R=[("ps", bg)], W=[("SG", ks)])
                                S.op(DVE, lambda: nc.vector.tensor_tensor(G[:, cch, xc(t)], psum[ba][:, 0:n],
                                                                          sgt[:, 0:n], ALU.mult),
                                     R=[("ps", ba), ("SG", ks)], W=[("G", cch, t)])
                    def build_dg(cch):
                        dg = DG[cch % 2]
                        for tap in range(CW):
                            wv = vcol(("cdw", j), tap * KC + cch)
                            if tap % 2 == 0:
                                S.op(DVE, lambda: nc.vector.tensor_scalar(dg[:, tap, :], IDENT[:, :], wv, None,
                                                                          op0=ALU.mult),
                                     R=["IDENT", "VEC"], W=[("DG", cch % 2, tap)])
                            else:
                                S.op(ACT, lambda: nc.scalar.activation(dg[:, tap, :], IDENT[:, :], AF.Identity, scale=wv),
                                     R=["IDENT", "VEC"], W=[("DG", cch % 2, tap)])

                    build_dg(0)
                    for cch in range(KC):
                        dg = DG[cch % 2]
                        for ti, t in enumerate(list(range(1, NT)) + [0]):
                            if ti == 1 and cch + 1 < KC:
                                build_dg(cch + 1)
                            bc = bank("C")
                            base = HALO + t * TW - (CW - 1)
                            for tap in range(CW):
                                mm(psum[bc][:, :], dg[:, tap, :], G[:, cch, base + tap:base + tap + TW],
                                   tap == 0, tap == CW - 1,
                                   R=[("DG", cch % 2, tap), ("G", cch, t), ("G", cch, t - 1)], W=[("ps", bc)])
                            S.op(ACT, lambda: nc.scalar.activation(Hn[:, cch, xc(t)], psum[bc][:, :], AF.Identity,
                                                                   bias=vcol(("cdb", j), cch)),
                                 R=[("ps", bc), "VEC"], W=[("Hn", cch, t)])
                S.barrier()
                with contextlib.ExitStack() as es3:
                    def asb3(name, shape, dt):
                        return es3.enter_context(nc.sbuf_tensor(uname(name), shape, dt))
                    SQ = [asb3(f"c3_sq{k}", [P, TW], BF16) for k in range(2)]
                    MU = asb3("c3_mu", [P, TW], F32)
                    M2 = asb3("c3_m2", [P, TW], F32)
                    RSTD = asb3("c3_rstd", [P, TW], F32)
                    MR = asb3("c3_mr", [P, TW], F32)
                    T1 = [asb3(f"c3_t1{k}", [P, TW], F32) for k in range(2)]
                    Y = [asb3(f"c3_y{k}", [P, KC, TW], BF16) for k in range(2)]
                    so0, so1 = ws.acquire(2)

                    def stats_norm(t):
                        b1, b2 = bank("A"), bank("B")
                        for c in range(KC):
                            q = SQ[c % 2]
                            S.op(ACT, lambda: nc.scalar.activation(q[:, :], Hn[:, c, xc(t)], AF.Square),
                                 R=[("Hn", c, t)], W=[("SQ3", c % 2)])
                            mm(psum[b1][:, :], ONES[:, :], Hn[:, c, xc(t)], c == 0, c == KC - 1,
                               R=[("Hn", c, t), "ONES"], W=[("ps", b1)])
                            mm(psum[b2][:, :], ONES[:, :], q[:, :], c == 0, c == KC - 1,
                               R=[("SQ3", c % 2), "ONES"], W=[("ps", b2)], inc=True)
                        S.op(DVE, lambda: nc.vector.tensor_scalar(MU[:, :], psum[b1][:, :], 1.0 / D, None, op0=ALU.mult),
                             R=[("ps", b1)], W=["MU"])
                        S.op(DVE, lambda: nc.vector.tensor_tensor(M2[:, :], MU[:, :], MU[:, :], ALU.mult),
                             R=["MU"], W=["M2"])
                        S.op(DVE, lambda: nc.vector.scalar_tensor_tensor(M2[:, :], psum[b2][:, :], 1.0 / D, M2[:, :],
                                                                         op0=ALU.mult, op1=ALU.subtract),
                             R=[("ps", b2), "M2"], W=["M2"])
                        S.op(ACT, lambda: nc.scalar.activation(RSTD[:, :], M2[:, :], AF.Ln, bias=EPSV[:, 0:1]),
                             R=["M2", "EPSV"], W=["RSTD"])
                        S.op(ACT, lambda: nc.scalar.activation(RSTD[:, :], RSTD[:, :], AF.Exp, scale=-0.5),
                             R=["RSTD"], W=["RSTD"])
                        S.op(DVE, lambda: nc.vector.tensor_tensor(MR[:, :], MU[:, :], RSTD[:, :], ALU.mult),
                             R=["MU", "RSTD"], W=["MR"])
                        Yt = Y[t % 2]
                        for c in range(KC):
                            t1 = T1[c % 2]
                            S.op(DVE, lambda: nc.vector.tensor_tensor(t1[:, :], Hn[:, c, xc(t)], RSTD[:, :], ALU.mult),
                                 R=[("Hn", c, t), "RSTD"], W=[("T1", c % 2)])
                            S.op(DVE, lambda: nc.vector.tensor_tensor(t1[:, :], t1[:, :], MR[:, :], ALU.subtract),
                                 R=[("T1", c % 2), "MR"], W=[("T1", c % 2)])
                            S.op(ACT, lambda: nc.scalar.activation(Yt[:, c, :], t1[:, :], AF.Silu,
                                                                   bias=vcol(("clb", j), c), scale=vcol(("clg", j), c)),
                                 R=[("T1", c % 2), "VEC"], W=[("Y", t % 2, c)])

                    def out3(t):
                        Yt = Y[t % 2]
                        for oc in range(KC):
                            so = so0 if oc < 4 else so1
                            O = slots[so]
                            bo = bank("C")
                            for kc in range(KC):
                                mm(psum[bo][:, :], O[:, kc, (oc % 4) * P:(oc % 4 + 1) * P], Yt[:, kc, :],
                                   kc == 0, kc == KC - 1, R=[("W", so), ("Y", t % 2, kc)], W=[("ps", bo)])
                            residual_add(oc, t, bo)

                    stats_norm(0)
                    for t in range(NT):
                        if t + 1 < NT:
                            stats_norm(t + 1)
                        out3(t)
                S.barrier()

        def retention(j, li):
            gam = [1.0 - 2.0 ** (-5.0 - h) for h in range(NH)]
            with contextlib.ExitStack() as es:
                def asb(name, shape, dt):
                    return es.enter_context(nc.sbuf_tensor(uname(name), shape, dt))
                COS = asb("r_cos", [P, T], F32)
                SIN = asb("r_sin", [P, T], F32)
                CM = asb("r_cm", [P, NH * 2 * CH], F32)
                with contextlib.ExitStack() as es2:
                    def asb2(name, shape, dt):
                        return es2.enter_context(nc.sbuf_tensor(uname(name), shape, dt))
                    SQ = [asb2(f"r_sq{k}", [P, TW], BF16) for k in range(2)]
                    RS = [asb2(f"r_rs{k}", [P, TW], F32) for k in range(2)]
                    PI_ = asb2("r_posi", [P, T], I32)
                    ANG = asb2("r_ang", [P, T], F32)
                    rmsnorm(("nmix", li), range(0, NT), (SQ, RS))
                    S.dma(SP, ld, CM[:, :], consts_d, W=["CM"])
                    S.dma(SP, ld, PI_[:, :], pos.partition_broadcast(P), W=["PI"])
                    S.op(DVE, lambda: nc.vector.tensor_copy(ANG[:, :], PI_[:, :]), R=["PI"], W=["ANG"])
                    S.op(DVE, lambda: nc.vector.tensor_scalar(ANG[:, :], ANG[:, :], vcol("invf"), None, op0=ALU.mult),
                         R=["ANG", "VEC"], W=["ANG"])
                    two_pi = 2.0 * math.pi
                    C1 = float(np.float32(6.28125))
                    C2 = float(two_pi - 6.28125)
                    RR = asb2("r_rr", [P, T], F32)
                    MM = asb2("r_mm", [P, T], F32)
                    S.op(DVE, lambda: nc.vector.tensor_scalar(MM[:, :], ANG[:, :], 1.0 / two_pi, None, op0=ALU.mult),
                         R=["ANG"], W=["MM"])
                    S.op(DVE, lambda: nc.vector.tensor_copy(PI_[:, :], MM[:, :]), R=["MM"], W=["PI"])
                    S.op(DVE, lambda: nc.vector.tensor_copy(MM[:, :], PI_[:, :]), R=["PI"], W=["MM"])
                    S.op(DVE, lambda: nc.vector.scalar_tensor_tensor(RR[:, :], MM[:, :], -C1, ANG[:, :],
                                                                     op0=ALU.mult, op1=ALU.add),
                         R=["MM", "ANG"], W=["RR"])
                    S.op(DVE, lambda: nc.vector.scalar_tensor_tensor(RR[:, :], MM[:, :], -C2, RR[:, :],
                                                                     op0=ALU.mult, op1=ALU.add),
                         R=["MM", "RR"], W=["RR"])
                    S.op(DVE, lambda: nc.vector.tensor_scalar(MM[:, :], RR[:, :], math.pi, -two_pi,
                                                              op0=ALU.is_gt, op1=ALU.mult), R=["RR"], W=["MM"])
                    S.op(DVE, lambda: nc.vector.tensor_tensor(SIN[:, :], RR[:, :], MM[:, :], ALU.add),
                         R=["RR", "MM"], W=["SIN"])
                    S.op(DVE, lambda: nc.vector.tensor_scalar(COS[:, :], RR[:, :], 0.5 * math.pi, None, op0=ALU.add),
                         R=["RR"], W=["COS"])
                    S.op(DVE, lambda: nc.vector.tensor_scalar(MM[:, :], COS[:, :], math.pi, -two_pi,
                                                              op0=ALU.is_gt, op1=ALU.mult), R=["COS"], W=["MM"])
                    S.op(DVE, lambda: nc.vector.tensor_tensor(COS[:, :], COS[:, :], MM[:, :], ALU.add),
                         R=["COS", "MM"], W=["COS"])
                    for nm, tt in (("SIN", SIN), ("COS", COS)):
                        S.op(DVE, lambda: nc.vector.tensor_scalar(tt[:, :], tt[:, :], -math.pi, math.pi,
                                                                  op0=ALU.max, op1=ALU.min), R=[nm], W=[nm])
                    S.op(ACT, lambda: nc.scalar.activation(SIN[:, :], SIN[:, :], AF.Sin), R=["SIN"], W=["SIN"])
                    S.op(ACT, lambda: nc.scalar.activation(COS[:, :], COS[:, :], AF.Sin), R=["COS"], W=["COS"])
                S.barrier()
                KV = asb("r_kv", [P, 4096], BF16)
                KT = KV[:, 0:1024].rearrange("p (a b) -> p a b", b=TW)
                KTM = KV[:, 1024:2048].rearrange("p (a b) -> p a b", b=DK)
                V = KV[:, 2048:4096].rearrange("p (a b) -> p a b", b=DV)
                KV_CELLS = [("KT", 0), ("KT", 1), "KTM"] + [("V", cl) for cl in range(4)]
                QT2 = [asb(f"r_qt{k}", [P, 2, TW], BF16) for k in range(2)]
                KTG = asb("r_ktg", [P, 4, DK], BF16)
                SGT2 = [asb(f"r_sg{k}", [P, 4, TW], BF16) for k in range(2)]
                YT = asb("r_yt", [P, 4, TW], BF16)
                RA = asb("r_ra", [P, TW], F32)
                RB = asb("r_rb", [P, TW], F32)
                RC = asb("r_rc", [P, TW], F32)
                ST = asb("r_st", [P, CH], BF16)
                SS = asb("r_ss", [P, 2, DV], F32)
                SBF = asb("r_sbf", [P, 2, DV], BF16)
                ON = asb("r_on", [P, DV], BF16)
                BST = asb("r_bst", [P, 8], F32)
                BMV = asb("r_bmv", [P, 4], F32)

                def maskT(h):
                    return CM[:, (2 * h) * CH:(2 * h + 1) * CH]

                def qdec(h):
                    return CM[:, (2 * h + 1) * CH:(2 * h + 2) * CH]

                def rot(ps1, ps2, t, out, dec):
                    cs, sn = COS[:, t * TW:(t + 1) * TW], SIN[:, t * TW:(t + 1) * TW]
                    R0 = [("ps", ps1), ("ps", ps2), "COS", "SIN"]
                    for half in range(2):
                        pa, pb = (ps1, ps2) if half == 0 else (ps2, ps1)
                        S.op(DVE, lambda: nc.vector.tensor_tensor(RA[:, :], psum[pa][:, :], cs, ALU.mult),
                             R=R0, W=["RA"])
                        S.op(DVE, lambda: nc.vector.tensor_tensor(RB[:, :], psum[pb][:, :], sn, ALU.mult),
                             R=R0, W=["RB"])
                        op = ALU.subtract if half == 0 else ALU.add
                        if dec is None:
                            S.op(DVE, lambda: nc.vector.tensor_tensor(out[0][:, half, :], RA[:, :], RB[:, :], op),
                                 R=["RA", "RB"], W=[(out[1], half)])
                        else:
                            S.op(DVE, lambda: nc.vector.tensor_tensor(RC[:, :], RA[:, :], RB[:, :], op),
                                 R=["RA", "RB"], W=["RC"])
                            for cl in range(4):
                                S.op(POOL, lambda: nc.gpsimd.tensor_tensor(out[0][:, half, cl * CH:(cl + 1) * CH],
                                                                           RC[:, cl * CH:(cl + 1) * CH], dec, ALU.mult),
                                     R=["RC", "CM"], W=[(out[1], half)])

                def k_and_v(h, t, sqk, sv, want_g, kpool="A", vpool="C"):
                    k_proj(h, t, sqk, kpool)
                    v_part(h, t, sv, vpool)
                    k_fin(h, t, want_g)

                def k_proj(h, t, sqk, kpool):
                    QK = slots[sqk]
                    b1, b2 = bank(kpool), bank(kpool)
                    for half, b in ((0, b1), (1, b2)):
                        for kc in range(KC):
                            mm(psum[b][:, :], QK[:, kc, DK + half * P:DK + (half + 1) * P], Hn[:, kc, xc(t)],
                               kc == 0, kc == KC - 1, R=[("W", sqk), ("Hn", kc, t)], W=[("ps", b)])
                    rot(b1, b2, t, (KT, "KT"), None)

                def k_fin(h, t, want_g):
                    for cl in range(4):
                        for dch in range(2):
                            S.op(PE, lambda: nc.tensor.transpose(psb[:, (cl * 2 + dch) * P:(cl * 2 + dch + 1) * P],
                                                                 KT[:, dch, cl * CH:(cl + 1) * CH], IDENT[:, :]),
                                 R=[("KT", dch), "IDENT"], W=["psb"], inc=(cl == 3 and dch == 1))
                    S.op(ACT, lambda: nc.scalar.activation(KV[:, 1024:2048], psb[:, :], AF.Identity,
                                                           scale=vcol(("kdec", h))),
                         R=["psb", "VEC"], W=["KTM"])
                    if want_g:
                        for cl in range(4):
                            S.op(ACT, lambda: nc.scalar.activation(KTG[:, cl, :], psb[:, cl * DK:(cl + 1) * DK], AF.Identity,
                                                                   scale=vcol(("gd", h), t * 4 + cl)),
                                 R=["psb", "VEC"], W=[("KTG", cl)])

                def v_part(h, t, sv, vpool):
                    Vw = slots[sv]
                    for cl in range(4):
                        bv = bank(vpool)
                        for kc in range(KC):
                            mm(psum[bv][:, :], Hn[:, kc, HALO + t * TW + cl * CH:HALO + t * TW + (cl + 1) * CH],
                               Vw[:, kc, :], kc == 0, kc == KC - 1, R=[("W", sv), ("Hn", kc, t)], W=[("ps", bv)])
                        S.op(ACT, lambda: nc.scalar.copy(V[:, cl, :], psum[bv][:, :]), R=[("ps", bv)], W=[("V", cl)])

                for h in range(NH):
                    sqk, sv = ws.acquire(2)
                    bs = [2, 3]
                    for t in range(NT):
                        k_and_v(h, t, sqk, sv, True)
                        for cl in range(4):
                            first = (t == 0 and cl == 0)
                            last = (t == NT - 1 and cl == 3)
                            for dch in range(2):
                                mm(psum[bs[dch]][:, :], KTG[:, cl, dch * P:(dch + 1) * P], V[:, cl, :], first, last,
                                   R=[("KTG", cl), ("V", cl)], W=[("ps", bs[dch])])
                        S.dma(SP, kvst, kvs[j][h][t], KV[:, :], R=KV_CELLS, W=[("kvs", j, h, t)])
                    for dch in range(2):
                        S.op(ACT, lambda: nc.scalar.copy(SS[:, dch, :], psum[bs[dch]][:, :]),
                             R=[("ps", bs[dch])], W=[("SS", dch)])
                    S.dma(SP, xch, msgS[j][h], SS[:].rearrange("p c e -> p (c e)"), R=[("SS", 0), ("SS", 1)],
                          W=[("msgS", j, h)])
                    S.collective(cc, [msgS[j][h]], [gathS[j][h]], R=[("msgS", j, h)], W=[("gathS", j, h)])

                seq = [(hh, tt) for hh in range(NH) for tt in range(NT)]

                def load_kv(idx):
                    if idx < len(seq):
                        hh, tt = seq[idx]
                        S.dma(SP, kvld, KV[:, :], kvs[j][hh][tt], R=[("kvs", j, hh, tt)], W=KV_CELLS)

                load_kv(0)
                for h in range(NH):
                    sq, sgw, so = ws.acquire(3)
                    QW, Gw = slots[sq], slots[sgw]
                    O = slots[so][:].rearrange("p a b -> p (a b)").rearrange("p (r n) -> p r n", n=D)
                    gC = gam[h] ** CH
                    S.dma(SP, xch, SS[:].rearrange("p c e -> p (c e)"), gathS[j][h][0:P, :], R=[("gathS", j, h)],
                          W=[("SS", 0), ("SS", 1)])
                    S.op(DVE, lambda: nc.vector.tensor_scalar(SS[:].rearrange("p c e -> p (c e)"),
                                                              SS[:].rearrange("p c e -> p (c e)"),
                                                              vcol("flag"), None, op0=ALU.mult),
                         R=[("SS", 0), ("SS", 1), "VEC"], W=[("SS", 0), ("SS", 1)])
                    for dch in range(2):
                        S.op(ACT, lambda: nc.scalar.copy(SBF[:, dch, :], SS[:, dch, :]),
                             R=[("SS", dch)], W=[("SBF", dch)])

                    def Pg(t, echs):
                        SGT = SGT2[t % 2]
                        for ech in echs:
                            bg = bank("C")
                            for kc in range(KC):
                                mm(psum[bg][:, :], Gw[:, kc, ech * P:(ech + 1) * P], Hn[:, kc, xc(t)],
                                   kc == 0, kc == KC - 1, R=[("W", sgw), ("Hn", kc, t)], W=[("ps", bg)])
                            S.op(ACT, lambda: nc.scalar.activation(SGT[:, ech, :], psum[bg][:, :], AF.Silu),
                                 R=[("ps", bg)], W=[("SGT", t % 2, ech)])

                    def Pq(t):
                        bq1, bq2 = bank("C"), bank("C")
                        for half, b in ((0, bq1), (1, bq2)):
                            for kc in range(KC):
                                mm(psum[b][:, :], QW[:, kc, half * P:(half + 1) * P], Hn[:, kc, xc(t)],
                                   kc == 0, kc == KC - 1, R=[("W", sq), ("Hn", kc, t)], W=[("ps", b)])
                        rot(bq1, bq2, t, (QT2[t % 2], ("QT", t % 2)), None)

                    Pq(0)
                    Pg(0, range(4))
                    for t in range(NT):
                        SGT = SGT2[t % 2]
                        sgpar = t % 2
                        QT = QT2[t % 2]
                        qpar = t % 2
                        def stageA(cl):
                            cs_ = slice(cl * CH, (cl + 1) * CH)
                            bsc = bank("D")
                            for dch in range(2):
                                mm(psum[bsc][:, 0:CH], KT[:, dch, cs_], QT[:, dch, cs_], dch == 0, dch == 1,
                                   R=[("KT", dch), (("QT", qpar), dch)], W=[("ps", bsc)])
                            for dch in range(2):
                                mm(psB[:, dch, :], KTM[:, cl, dch * P:(dch + 1) * P], V[:, cl, :], True, True,
                                   R=["KTM", ("V", cl)], W=[("ps", 2 + dch)])
                            S.op(DVE, lambda: nc.vector.tensor_tensor(ST[:, :], psum[bsc][:, 0:CH], maskT(h), ALU.mult),
                                 R=[("ps", bsc), "CM"], W=["ST"])
                            bo = bank("A")
                            mm(psum[bo][:, :], ST[:, :], V[:, cl, :], True, False, R=["ST", ("V", cl)], W=[("ps", bo)])
                            for dch in range(2):
                                mm(psum[bo][:, :], QT[:, dch, cs_], SBF[:, dch, :], False, dch == 1,
                                   R=[(("QT", qpar), dch), ("SBF", dch)], W=[("ps", bo)])
                            S.op(DVE, lambda: nc.vector.scalar_tensor_tensor(SS[:, :, :], SS[:, :, :], float(gC),
                                                                             psB[:, :, :],
                                                                             op0=ALU.mult, op1=ALU.add),
                                 R=[("ps", 2), ("ps", 3), ("SS", 0), ("SS", 1)], W=[("SS", 0), ("SS", 1)])
                            S.op(ACT, lambda: nc.scalar.copy(SBF[:, :, :], SS[:, :, :]),
                                 R=[("SS", 0), ("SS", 1)], W=[("SBF", 0), ("SBF", 1)])
                            return bo

                        def stageB(cl, bo):
                            cs_ = slice(cl * CH, (cl + 1) * CH)
                            S.op(DVE, lambda: nc.vector.bn_stats(BST[:, 0:6], psum[bo][:, :]), R=[("ps", bo)], W=["BST"])
                            S.op(DVE, lambda: nc.vector.bn_aggr(BMV[:, 0:2], BST[:, 0:6]), R=["BST"], W=["BMV"])
                            S.op(ACT, lambda: nc.scalar.activation(BMV[:, 2:3], BMV[:, 1:2], AF.Sqrt, bias=vcol(("epsn", h))),
                                 R=["BMV", "VEC"], W=["BMV"])
                            S.op(DVE, lambda: nc.vector.reciprocal(BMV[:, 2:3], BMV[:, 2:3]), R=["BMV"], W=["BMV"])
                            S.op(DVE, lambda: nc.vector.tensor_scalar(ON[:, :], psum[bo][:, :], BMV[:, 0:1], BMV[:, 2:3],
                                                                      op0=ALU.subtract, op1=ALU.mult),
                                 R=[("ps", bo), "BMV"], W=["ON"])
                            for ech in range(4):
                                S.op(PE, lambda: nc.tensor.transpose(psb[:, ech * P:(ech + 1) * P],
                                                                     ON[:, ech * P:(ech + 1) * P], IDENT[:, :]),
                                     R=["ON", "IDENT"], W=["psb"], inc=(ech == 3))
                            for ech in range(4):
                                S.op(DVE, lambda: nc.vector.scalar_tensor_tensor(
                                    YT[:, ech, cs_], psb[:, ech * P:(ech + 1) * P], vcol(("gng", j), h * 4 + ech),
                                    SGT[:, ech, cs_], op0=ALU.mult, op1=ALU.mult),
                                    R=["psb", "VEC", ("SGT", sgpar, ech)], W=[("YT", ech)])

                        nxt = t + 1 < NT
                        bos = {0: stageA(0)}
                        if nxt:
                            Pg(t + 1, (0, 1))
                        bos[1] = stageA(1)
                        if nxt:
                            Pg(t + 1, (2, 3))
                        stageB(0, bos[0])
                        bos[2] = stageA(2)
                        stageB(1, bos[1])
                        bos[3] = stageA(3)
                        load_kv(h * NT + t + 1)
                        if nxt:
                            Pq(t + 1)
                        stageB(2, bos[2])
                        stageB(3, bos[3])
                        for oc in range(KC):
                            bo2 = bank("C")
                            for ech in range(4):
                                mm(psum[bo2][:, :], O[:, ech, oc * P:(oc + 1) * P], YT[:, ech, :], ech == 0, ech == 3,
                                   R=[("W", so), ("YT", ech)], W=[("ps", bo2)])
                            residual_add(oc, t, bo2)
                S.barrier()

        XT = sb("XT", [P, KC, HALO], F32)
        first = True
        for li in cfg.layers:
            if li % 2 == 0:
                if not first:
                    exchange_halo(XT)
                conv_module(li // 2, li)
            else:
                retention(li // 2, li)
            exchange_halo(XT)
            ffn(li)
            first = False

        if cfg.final_norm:
            with contextlib.ExitStack() as es:
                SQ = [es.enter_context(nc.sbuf_tensor(uname(f"n_sq{k}"), [P, TW], BF16)) for k in range(2)]
                RSF = [es.enter_context(nc.sbuf_tensor(f"n_rs{k}", [P, TW], F32)) for k in range(2)]
                for t in range(NT):
                    RS = RSF[t % 2]
                    b = bank("D") if t % 2 == 0 else bank("C")
                    for c in range(KC):
                        q = SQ[c % 2]
                        S.op(ACT, lambda: nc.scalar.activation(q[:, :], X[:, c, xc(t)], AF.Square),
                             R=[("X", c, t)], W=[("SQ", c % 2)])
                        mm(psum[b][:, :], ONES[:, :], q[:, :], c == 0, c == KC - 1,
                           R=[("SQ", c % 2), "ONES"], W=[("ps", b)], inc=True)
                    S.op(ACT, lambda: nc.scalar.activation(RS[:, :], psum[b][:, :], AF.Ln, bias=EPSV[:, 0:1],
                                                           scale=1.0 / D), R=[("ps", b), "EPSV"], W=[("RSF", t % 2)])
                    S.op(ACT, lambda: nc.scalar.activation(RS[:, :], RS[:, :], AF.Exp, scale=-0.5), R=[("RSF", t % 2)], W=[("RSF", t % 2)])
                    for c in range(KC):
                        S.op(DVE, lambda: nc.vector.scalar_tensor_tensor(
                            X[:, c, xc(t)], X[:, c, xc(t)], vcol("nfin", c), RS[:, :], op0=ALU.mult, op1=ALU.mult),
                            R=[("X", c, t), ("RSF", t % 2), "VEC"], W=[("X", c, t)])
                    if cfg.final_norm:
                        S.dma(SP, st_out, outT3[:, :, t * TW:(t + 1) * TW], X[:, :, xc(t)],
                              R=[("X", c, t) for c in range(KC)], W=[("out", t)])
                S.barrier()
        if not cfg.final_norm:
            for t in range(NT):
                S.dma(SP, st_out, outT3[:, :, t * TW:(t + 1) * TW], X[:, :, xc(t)],
                      R=[("X", c, t) for c in range(KC)], W=[("out", t)])
        S.finish()
        assert ws.next == len(ws.plan), (ws.next, len(ws.plan))
    return nc


def pack_vecs(inp, T, odd):
    voff, NV = vec_layout()
    v = np.zeros((P, NV), np.float32)

    def put(name, arr, nchunk):
        v[:, voff[name]:voff[name] + nchunk] = np.asarray(arr, np.float32).reshape(nchunk, P).T
    for i in range(4):
        put(("nmix", i), inp["norm_mix_g"][i], KC)
        put(("nffn", i), inp["norm_ffn_g"][i], KC)
        for tap in range(3):
            v[:, voff[("fdw", i)] + tap * NFC: voff[("fdw", i)] + (tap + 1) * NFC] = \
                np.asarray(inp["ffn_dw_w"][i][tap], np.float32).reshape(NFC, P).T
        put(("fdb", i), inp["ffn_dw_b"][i], NFC)
    put("nfin", inp["final_g"], KC)
    for j in range(2):
        for tap in range(CW):
            v[:, voff[("cdw", j)] + tap * KC: voff[("cdw", j)] + (tap + 1) * KC] = \
                np.asarray(inp["conv_dw_w"][j][tap], np.float32).reshape(KC, P).T
        put(("cdb", j), inp["conv_dw_b"][j], KC)
        put(("clg", j), inp["conv_ln_g"][j], KC)
        put(("clb", j), inp["conv_ln_b"][j], KC)
        put(("gng", j), inp["ret_gn_g"][j], 16)
    v[:, voff["flag"]] = 1.0 if odd else 0.0
    jj = np.arange(P, dtype=np.float32)
    v[:, voff["invf"]] = (1.0 / (np.float32(10000.0) ** (np.arange(0, DK, 2, dtype=np.float32) / np.float32(DK)))).astype(np.float32)
    scale = DK ** -0.5
    m = np.arange(CH, dtype=np.float64)
    for h in range(NH):
        g = 1.0 - 2.0 ** (-5.0 - h)
        v[:, voff[("kdec", h)]] = (scale * g ** (CH - 1 - m)).astype(np.float32)
        v[:, voff[("qc", h)]] = (g ** (m + 1.0)).astype(np.float32)
        v[:, voff[("qc2", h)]] = (g ** (2.0 * (m + 1.0))).astype(np.float32)
        v[:, voff[("epsn", h)]] = (EPS / g ** (2.0 * (m + 1.0))).astype(np.float32)
        for c in range(T // CH):
            v[:, voff[("gd", h)] + c] = (scale * g ** (T - 1 - (c * CH + m))).astype(np.float32)
    return v


def make_consts():
    cm = np.zeros((P, NH * 2 * CH), np.float32)
    m = np.arange(CH, dtype=np.float64)[:, None]
    n = np.arange(CH, dtype=np.float64)[None, :]
    for h in range(NH):
        g = 1.0 - 2.0 ** (-5.0 - h)
        scale = DK ** -0.5
        cm[:, (2 * h) * CH:(2 * h + 1) * CH] = np.where(m <= n, scale * g ** (-(m + 1.0)), 0.0)
        cm[:, (2 * h + 1) * CH:(2 * h + 2) * CH] = np.broadcast_to(g ** (n + 1.0), (P, CH))
    return cm


_PROG_CACHE = {}


def run_layers(inputs, layers, final_norm, x_override=None):
    x = np.asarray(inputs["x"], np.float32) if x_override is None else x_override
    B, SEQ, _ = x.shape
    T = SEQ // 2
    key = (T, tuple(layers), final_norm)
    if key not in _PROG_CACHE:
        _PROG_CACHE[key] = build_program(Cfg(T, list(layers), final_norm))
    nc = _PROG_CACHE[key]
    posi = np.asarray(inputs["positions"], np.int32)
    shared = {k: np.ascontiguousarray(np.asarray(inputs[k], np.float32)) for k in
              ("conv_w_in", "conv_w_out", "ret_w_in", "ret_w_out", "ffn_w_in", "ffn_w_out")}
    cm = make_consts()
    ident = np.eye(P, dtype=np.float32)
    in_maps = []
    for core in range(8):
        b, s = core // 2, core % 2
        xT = np.zeros((D, HALO + T), np.float32)
        xT[:, HALO:] = x[b, s * T:(s + 1) * T, :].T
        if s == 1:
            xT[:, :HALO] = x[b, T - HALO:T, :].T
        m = dict(shared)
        m["xT"] = xT
        m["pos"] = np.ascontiguousarray(posi[b, s * T:(s + 1) * T].reshape(1, T))
        m["vecs"] = pack_vecs(inputs, T, s == 1)
        m["cmask"] = cm
        m["ident"] = ident
        in_maps.append(m)
    res = run_bass_kernel_spmd(nc, in_maps, core_ids=list(range(8)))
    out = np.zeros((B, SEQ, D), np.float32)
    for core in range(8):
        b, s = core // 2, core % 2
        out[b, s * T:(s + 1) * T, :] = res.results[core]["outT"].T
    return out


def kernel(**inputs):
    return run_layers(inputs, [0, 1, 2, 3], True)
```

```python
import contextlib
import math
import numpy as np
import concourse.bass as bass
import concourse.mybir as mybir
from concourse.bass_utils import run_bass_kernel_spmd

F32 = mybir.dt.float32
BF16 = mybir.dt.bfloat16
I32 = mybir.dt.int32
AF = mybir.ActivationFunctionType
ALU = mybir.AluOpType

P = 128
D = 1024
KC = 8
FF = 2816
NFC = 22
CW = 31
HALO = 32
NH = 4
DK = 256
DV = 512
CH = 128
EPS = 1e-6
TW = 512
NSLOT = 5
EPOCH = 12000
PAIRS = [[0, 1], [2, 3], [4, 5], [6, 7]]


class Eng:
    def __init__(self, sched, name, handle, unit=1, is_pe=False, is_dma=False):
        self.s = sched
        self.name = name
        self.h = handle
        self.unit = unit
        self.is_pe = is_pe
        self.is_dma = is_dma
        self.count = 0
        self.sems = []
        self.waited = {}
        self.max_waited = 0

    def sem_val(self, idx):
        e = (idx - 1) // EPOCH
        while len(self.sems) <= e:
            self.sems.append(self.s.new_sem(f"{self.name}_{len(self.sems)}"))
        return self.sems[e], (idx - e * EPOCH) * self.unit


class Cell:
    __slots__ = ("w", "r")

    def __init__(self):
        self.w = None
        self.r = {}


class Sched:
    def __init__(self, nc, stack):
        self.nc = nc
        self.stack = stack
        self.cells = {}
        self.nsem = 0
        self.pe = Eng(self, "pe", nc.tensor, is_pe=True)
        self.act = Eng(self, "act", nc.scalar)
        self.dve = Eng(self, "dve", nc.vector)
        self.pool = Eng(self, "pool", nc.gpsimd)
        self.sp = Eng(self, "sp", nc.sync)
        self.engines = [self.pe, self.act, self.dve, self.pool, self.sp]
        self.streams = []

    def new_sem(self, name):
        self.nsem += 1
        return self.stack.enter_context(self.nc.semaphore(name))

    def stream(self, name, unit=16):
        st = Eng(self, name, None, unit=unit, is_dma=True)
        self.streams.append(st)
        return st

    def cell(self, k):
        c = self.cells.get(k)
        if c is None:
            c = self.cells[k] = Cell()
        return c

    def _deps(self, R, W):
        deps = {}

        def add(e, idx):
            if deps.get(e, 0) < idx:
                deps[e] = idx
        for k in R:
            c = self.cell(k)
            if c.w is not None:
                add(*c.w)
        for k in W:
            c = self.cell(k)
            if c.w is not None:
                add(*c.w)
            for e, idx in c.r.values():
                add(e, idx)
        return deps

    def _wait(self, eng, deps):
        for e, idx in deps.items():
            if e is eng and eng.is_pe:
                continue
            if e.is_dma:
                idx = e.count
            if eng.waited.get(e.name, 0) >= idx:
                continue
            sem, val = e.sem_val(idx)
            eng.h.wait_ge(sem, val)
            eng.waited[e.name] = idx
            if e.is_dma and e.max_waited < idx:
                e.max_waited = idx

    def _wait_exact(self, eng, deps, own_stream):
        rest = {e: i for e, i in deps.items() if e is not own_stream}
        self._wait(eng, rest)
        if own_stream in deps:
            idx = deps[own_stream]
            if eng.waited.get(own_stream.name, 0) < idx:
                sem, val = own_stream.sem_val(idx)
                eng.h.wait_ge(sem, val)
                eng.waited[own_stream.name] = idx

    def _record(self, who, idx, R, W):
        for k in R:
            self.cell(k).r[who.name] = (who, idx)
        for k in W:
            c = self.cell(k)
            c.w = (who, idx)
            c.r = {}

    def op(self, eng, fn, R=(), W=(), inc=True):
        self._wait(eng, self._deps(R, W))
        inst = fn()
        if inc:
            eng.count += 1
            sem, val = eng.sem_val(eng.count)
            inst.then_inc(sem, 1)
            idx = eng.count
        else:
            idx = eng.count + 1
        self._record(eng, idx, R, W)
        return inst

    def dma(self, issuer, stream, out, in_, R=(), W=(), **kw):
        deps = self._deps(R, W)
        if stream.max_waited > 0:
            deps[stream] = max(deps.get(stream, 0), stream.max_waited)
        self._wait_exact(issuer, deps, stream)
        stream.count += 1
        sem, val = stream.sem_val(stream.count)
        issuer.h.dma_start(out=out, in_=in_, **kw).then_inc(sem, 16)
        self._record(stream, stream.count, R, W)

    def collective(self, stream, ins, outs, R=(), W=()):
        issuer = self.pool
        deps = self._deps(R, W)
        if stream.max_waited > 0:
            deps[stream] = max(deps.get(stream, 0), stream.max_waited)
        self._wait_exact(issuer, deps, stream)
        stream.count += 1
        sem, val = stream.sem_val(stream.count)
        issuer.h.collective_compute("AllGather", ALU.bypass, replica_groups=PAIRS,
                                    ins=ins, outs=outs).then_inc(sem, 1)
        self._record(stream, stream.count, R, W)

    def barrier(self):
        for eng in (self.pe, self.act, self.dve, self.pool, self.sp):
            deps = {}
            for e in self.engines + self.streams:
                if e.count > 0 and e is not eng and not e.name.startswith("w"):
                    deps[e] = e.count
            saved = eng.is_pe
            self._wait(eng, deps)

    def finish(self):
        for eng in (self.sp,):
            deps = {e: e.count for e in self.streams + self.engines if e.count > 0 and e is not eng}
            self._wait(eng, deps)


class WStream:
    def __init__(self, S, slots):
        self.S = S
        self.slots = slots
        self.plan = []
        self.issued = 0
        self.next = 0
        self.streams = [S.stream(f"w{i}") for i in range(len(slots))]

    def add(self, loads):
        self.plan.append(loads)

    def _issue_upto(self, limit):
        S = self.S
        while self.issued < min(limit, len(self.plan)):
            i = self.issued
            s = i % len(self.slots)
            for dst_fn, src in self.plan[i]:
                S.dma(S.pool, self.streams[s], dst_fn(self.slots[s]), src, R=(), W=[("W", s)])
            self.issued += 1

    def acquire(self, n):
        c = self.next
        self._issue_upto(c + len(self.slots))
        assert self.issued >= c + n, (self.issued, c, n)
        res = [((c + j) % len(self.slots)) for j in range(n)]
        self.next += n
        return res


def split_groups(n, g):
    out = []
    i = 0
    while i < n:
        m = min(g, n - i)
        out.append(list(range(i, i + m)))
        i += m
    return out


class Cfg:
    def __init__(self, T, layers, final_norm=True):
        self.T = T
        self.NT = T // TW
        self.NCH = T // CH
        self.layers = layers
        self.final_norm = final_norm


def vec_layout():
    off = {}
    n = 0

    def add(name, cnt):
        nonlocal n
        off[name] = n
        n += cnt
    for i in range(4):
        add(("nmix", i), KC)
        add(("nffn", i), KC)
        add(("fdw", i), 3 * NFC)
        add(("fdb", i), NFC)
    add("nfin", KC)
    for j in range(2):
        add(("cdw", j), CW * KC)
        add(("cdb", j), KC)
        add(("clg", j), KC)
        add(("clb", j), KC)
        add(("gng", j), 16)
    add("flag", 1)
    add("invf", 1)
    for h in range(NH):
        add(("kdec", h), 1)
        add(("gd", h), 16)
        add(("qc", h), 1)
        add(("qc2", h), 1)
        add(("epsn", h), 1)
    return off, n


def build_program(cfg):
    T, NT, NCH = cfg.T, cfg.NT, cfg.NCH
    TX = HALO + T
    nc = bass.Bass("TRN2", target_bir_lowering=False)
    voff, NV = vec_layout()

    def din(name, shape, dt=F32):
        return nc.dram_tensor(name, shape, dt, kind="ExternalInput").ap()

    xT = din("xT", [D, TX])
    pos = din("pos", [1, T], I32)
    vecs_d = din("vecs", [P, NV])
    consts_d = din("cmask", [P, NH * 2 * CH])
    ident_d = din("ident", [P, P])
    conv_w_in = din("conv_w_in", [2, D, 2 * D])
    conv_w_out = din("conv_w_out", [2, D, D])
    ret_w_in = din("ret_w_in", [2, D, 6144])
    ret_w_out = din("ret_w_out", [2, 2048, D])
    ffn_w_in = din("ffn_w_in", [4, D, 2 * FF])
    ffn_w_out = din("ffn_w_out", [4, FF, D])
    outT = nc.dram_tensor("outT", [D, T], F32, kind="ExternalOutput").ap()

    n_x_exch = 8
    msgX = [nc.dram_tensor(f"msgX{i}", [P, KC * HALO], F32, kind="Internal").ap() for i in range(n_x_exch)]
    gathX = [nc.dram_tensor(f"gathX{i}", [2 * P, KC * HALO], F32, kind="Internal").ap() for i in range(n_x_exch)]
    msgS = [[nc.dram_tensor(f"msgS{i}_{h}", [P, 2 * DV], F32, kind="Internal").ap() for h in range(NH)] for i in range(2)]
    gathS = [[nc.dram_tensor(f"gathS{i}_{h}", [2 * P, 2 * DV], F32, kind="Internal").ap() for h in range(NH)] for i in range(2)]

    kvs = [[[nc.dram_tensor(f"kvs{jj}_{h}_{t}", [P, 4096], BF16, kind="Internal").ap() for t in range(NT)]
            for h in range(NH)] for jj in range(2)]

    stack = contextlib.ExitStack()
    with stack:
        S = Sched(nc, stack)
        PE, ACT, DVE, POOL, SP = S.pe, S.act, S.dve, S.pool, S.sp

        def sb(name, shape, dt):
            return stack.enter_context(nc.sbuf_tensor(name, shape, dt))

        uniq = [0]

        def uname(name):
            uniq[0] += 1
            return f"{name}_u{uniq[0]}"

        X = sb("X", [P, KC, TX], F32)
        Hn = sb("Hn", [P, KC, TX], BF16)
        slots = [sb(f"wslot{i}", [P, KC, TW], BF16) for i in range(NSLOT)]
        VEC = sb("VEC", [P, NV], F32)
        IDENT = sb("IDENT", [P, P], BF16)
        ONES = sb("ONES", [P, P], BF16)
        EPSV = sb("EPSV", [P, 1], F32)
        ps_lo = [stack.enter_context(nc.psum_tensor(f"ps{i}", [P, TW], F32)) for i in range(2)]
        psB = stack.enter_context(nc.psum_tensor("psB", [P, 2, TW], F32))
        ps_hi = [stack.enter_context(nc.psum_tensor(f"ps{i}", [P, TW], F32)) for i in range(4, 7)]
        psum = [ps_lo[0][:, :], ps_lo[1][:, :], psB[:, 0, :], psB[:, 1, :]] + [t_[:, :] for t_ in ps_hi]
        psb = stack.enter_context(nc.psum_tensor("psb", [P, 2 * TW], BF16))
        ps_rr = {"A": [0, 1], "B": [2, 3], "C": [4, 5], "D": [6]}
        ps_ctr = {k: 0 for k in ps_rr}

        def bank(pool):
            lst = ps_rr[pool]
            b = lst[ps_ctr[pool] % len(lst)]
            ps_ctr[pool] += 1
            return b

        ws = WStream(S, slots)
        ld = S.stream("ld")
        ldp = S.stream("ldp")
        st_out = S.stream("st")
        xch = S.stream("xch")
        kvst = S.stream("kvst")
        kvld = S.stream("kvld")
        cc = S.stream("cc", unit=1)

        def xc(t, lo=0, hi=None):
            if t < 0:
                return slice(0, HALO)
            base = HALO + t * TW
            return slice(base + lo, base + (TW if hi is None else hi))

        def vcol(name, i=0):
            o = voff[name] + i
            return VEC[:, o:o + 1]

        def wsrc(w2d, r0, nr, c0, ncol):
            return w2d[r0 * P:(r0 + nr) * P, c0:c0 + ncol].rearrange("(r p) n -> p r n", p=P)

        ffn_groups = split_groups(NFC, 4)

        def plan_conv(j):
            w_in, w_out = conv_w_in[j], conv_w_out[j]
            for half in range(2):
                ws.add([(lambda s: s[:, :, :], wsrc(w_in, 0, KC, half * 512, 512))])
                ws.add([(lambda s: s[:, :, :], wsrc(w_in, 0, KC, D + half * 512, 512))])
            for half in range(2):
                ws.add([(lambda s: s[:, :, :], wsrc(w_out, 0, KC, half * 512, 512))])

        def plan_ffn(i):
            w_in, w_out = ffn_w_in[i], ffn_w_out[i]
            for g in ffn_groups:
                n = len(g) * P
                ws.add([(lambda s, n=n: s[:, :, 0:n], wsrc(w_in, 0, KC, g[0] * P, n))])
                ws.add([(lambda s, n=n: s[:, :, 0:n], wsrc(w_in, 0, KC, FF + g[0] * P, n))])
                ws.add([(lambda s, g=g: s[:].rearrange("p a b -> p (a b)")[:, 0:len(g) * D]
                         .rearrange("p (r n) -> p r n", n=D),
                         wsrc(w_out, g[0], len(g), 0, D))])

        def plan_ret(j):
            w_in, w_out = ret_w_in[j], ret_w_out[j]

            for h in range(NH):
                ws.add([(lambda s: s[:, :, DK:2 * DK], wsrc(w_in, 0, KC, D + h * DK, DK))])
                ws.add([(lambda s: s[:, :, :], wsrc(w_in, 0, KC, 2 * D + h * DV, DV))])
            for h in range(NH):
                ws.add([(lambda s: s[:, :, 0:DK], wsrc(w_in, 0, KC, h * DK, DK))])
                ws.add([(lambda s: s[:, :, :], wsrc(w_in, 0, KC, 4 * D + h * DV, DV))])
                ws.add([(lambda s: s[:].rearrange("p a b -> p (a b)").rearrange("p (r n) -> p r n", n=D),
                         wsrc(w_out, h * 4, 4, 0, D))])

        for li in cfg.layers:
            if li % 2 == 0:
                plan_conv(li // 2)
            else:
                plan_ret(li // 2)
            plan_ffn(li)

        xT3 = xT.rearrange("(c p) n -> p c n", p=P)
        outT3 = outT.rearrange("(c p) n -> p c n", p=P)
        for t in list(range(NT)) + [-1]:
            S.dma(SP, ld, X[:, :, xc(t)], xT3[:, :, xc(t)], W=[("X", c, t) for c in range(KC)])
        S.dma(SP, ld, VEC[:, :], vecs_d, W=["VEC"])
        S.dma(POOL, ldp, IDENT[:, :], ident_d, W=["IDENT"])
        S.op(DVE, lambda: nc.vector.memset(ONES[:, :], 1.0), W=["ONES"])
        S.op(DVE, lambda: nc.vector.memset(EPSV[:, :], float(EPS)), W=["EPSV"])

        def mm(out, lhsT, rhs, start, stop, R, W, inc=None):
            S.op(PE, lambda: nc.tensor.matmul(out, lhsT, rhs, start=start, stop=stop), R=R, W=W,
                 inc=(stop if inc is None else inc))

        def rmsnorm(gname, tiles, scr):
            SQ, RS2 = scr
            for ti, t in enumerate(tiles):
                n = HALO if t < 0 else TW
                RS = RS2[ti % 2]
                rk = ("RS", ti % 2)
                b = bank("D") if ti % 2 == 0 else bank("C")
                for c in range(KC):
                    q = SQ[c % 2]
                    S.op(ACT, lambda: nc.scalar.activation(q[:, 0:n], X[:, c, xc(t)], AF.Square),
                         R=[("X", c, t)], W=[("SQ", c % 2)])
                    mm(psum[b][:, 0:n], ONES[:, :], q[:, 0:n], c == 0, c == KC - 1,
                       R=[("SQ", c % 2), "ONES"], W=[("ps", b)], inc=True)
                S.op(ACT, lambda: nc.scalar.activation(RS[:, 0:n], psum[b][:, 0:n], AF.Ln, bias=EPSV[:, 0:1],
                                                       scale=1.0 / D),
                     R=[("ps", b), "EPSV"], W=[rk])
                S.op(ACT, lambda: nc.scalar.activation(RS[:, 0:n], RS[:, 0:n], AF.Exp, scale=-0.5), R=[rk], W=[rk])
                for c in range(KC):
                    S.op(DVE, lambda: nc.vector.scalar_tensor_tensor(
                        Hn[:, c, xc(t)], X[:, c, xc(t)], vcol(gname, c), RS[:, 0:n],
                        op0=ALU.mult, op1=ALU.mult),
                        R=[("X", c, t), rk, "VEC"], W=[("Hn", c, t)])

        def residual_add(oc, t, b):
            S.op(DVE, lambda: nc.vector.tensor_tensor(X[:, oc, xc(t)], X[:, oc, xc(t)], psum[b][:, :], ALU.add),
                 R=[("ps", b), ("X", oc, t)], W=[("X", oc, t)])

        xcount = [0]

        def exchange_halo(XT):
            i = xcount[0]
            xcount[0] += 1
            tl = NT - 1
            for c in range(KC):
                S.op(ACT, lambda: nc.scalar.copy(XT[:, c, :], X[:, c, HALO + T - HALO:HALO + T]),
                     R=[("X", c, tl)], W=["XT"])
            S.dma(SP, xch, msgX[i], XT[:].rearrange("p c h -> p (c h)"), R=["XT"], W=[("msgX", i)])
            S.collective(cc, [msgX[i]], [gathX[i]], R=[("msgX", i)], W=[("gathX", i)])
            S.dma(SP, xch, XT[:].rearrange("p c h -> p (c h)"), gathX[i][0:P, :], R=[("gathX", i)], W=["XT"])
            for c in range(KC):
                S.op(DVE, lambda: nc.vector.tensor_scalar(X[:, c, 0:HALO], XT[:, c, :], vcol("flag"), None,
                                                          op0=ALU.mult),
                     R=["XT", "VEC"], W=[("X", c, -1)])

        def ffn(i):
            with contextlib.ExitStack() as es:
                def asb(name, shape, dt):
                    return es.enter_context(nc.sbuf_tensor(uname(name), shape, dt))
                SQ = [asb(f"f_sq{k}", [P, TW], BF16) for k in range(2)]
                RS = [asb(f"f_rs{k}", [P, TW], F32) for k in range(2)]
                AB = [asb(f"f_ab{k}", [P, TW + 2], F32) for k in range(2)]
                CB = [asb(f"f_cb{k}", [P, TW], F32) for k in range(2)]
                SL = [asb(f"f_sl{k}", [P, TW], F32) for k in range(2)]
                Z = [asb(f"f_z{k}", [P, 4, TW], BF16) for k in range(2)]
                TAIL = asb("f_tail", [P, NFC, 2], F32)
                rmsnorm(("nffn", i), list(range(0, NT)) + [-1], (SQ, RS))
                k_ab = 0
                for g in ffn_groups:
                    sa, su, so = ws.acquire(3)
                    A, U = slots[sa], slots[su]
                    O = slots[so][:].rearrange("p a b -> p (a b)")[:, 0:len(g) * D].rearrange("p (r n) -> p r n", n=D)
                    def out_proj(tt):
                        Zo = Z[tt % 2]
                        for oc in range(KC):
                            bo = bank("C")
                            for j in range(len(g)):
                                mm(psum[bo][:, :], O[:, j, oc * P:(oc + 1) * P], Zo[:, j, :],
                                   j == 0, j == len(g) - 1, R=[("W", so), ("Z", tt % 2, j)], W=[("ps", bo)])
                            residual_add(oc, tt, bo)

                    for t in range(NT):
                        Zt = Z[t % 2]
                        for j, fc in enumerate(g):
                            ba, bu = bank("A"), bank("B")
                            hn_r = [("Hn", kc, t) for kc in range(KC)]
                            for kc in range(KC):
                                mm(psum[ba][:, :], A[:, kc, j * P:(j + 1) * P], Hn[:, kc, xc(t)],
                                   kc == 0, kc == KC - 1, R=[("W", sa), ("Hn", kc, t)], W=[("ps", ba)])
                            for kc in range(KC):
                                mm(psum[bu][:, :], U[:, kc, j * P:(j + 1) * P], Hn[:, kc, xc(t)],
                                   kc == 0, kc == KC - 1, R=[("W", su), ("Hn", kc, t)], W=[("ps", bu)])
                            ab = AB[k_ab % 2]
                            cb = CB[k_ab % 2]
                            sl = SL[k_ab % 2]
                            kab = k_ab % 2
                            k_ab += 1
                            if t == 0:
                                bh = bank("D")
                                for kc in range(KC):
                                    mm(psum[bh][:, 0:2], A[:, kc, j * P:(j + 1) * P], Hn[:, kc, HALO - 2:HALO],
                                       kc == 0, kc == KC - 1, R=[("W", sa), ("Hn", kc, -1)], W=[("ps", bh)])
                                S.op(ACT, lambda: nc.scalar.copy(ab[:, 0:2], psum[bh][:, 0:2]),
                                     R=[("ps", bh)], W=[("AB", kab)])
                            else:
                                S.op(ACT, lambda: nc.scalar.copy(ab[:, 0:2], TAIL[:, fc, :]),
                                     R=[("TAIL", fc)], W=[("AB", kab)])
                            S.op(ACT, lambda: nc.scalar.copy(ab[:, 2:TW + 2], psum[ba][:, :]),
                                 R=[("ps", ba)], W=[("AB", kab)])
                            if t < NT - 1:
                                S.op(ACT, lambda: nc.scalar.copy(TAIL[:, fc, :], ab[:, TW:TW + 2]),
                                     R=[("AB", kab)], W=[("TAIL", fc)])
                            w0 = vcol(("fdw", i), 0 * NFC + fc)
                            w1 = vcol(("fdw", i), 1 * NFC + fc)
                            w2 = vcol(("fdw", i), 2 * NFC + fc)
                            bb = vcol(("fdb", i), fc)
                            S.op(DVE, lambda: nc.vector.tensor_scalar(cb[:, :], ab[:, 2:TW + 2], w2, bb,
                                                                      op0=ALU.mult, op1=ALU.add),
                                 R=[("AB", kab), "VEC"], W=[("CB", kab)])
                            S.op(DVE, lambda: nc.vector.scalar_tensor_tensor(cb[:, :], ab[:, 1:TW + 1], w1, cb[:, :],
                                                                             op0=ALU.mult, op1=ALU.add),
                                 R=[("AB", kab), ("CB", kab)], W=[("CB", kab)])
                            S.op(DVE, lambda: nc.vector.scalar_tensor_tensor(cb[:, :], ab[:, 0:TW], w0, cb[:, :],
                                                                             op0=ALU.mult, op1=ALU.add),
                                 R=[("AB", kab), ("CB", kab)], W=[("CB", kab)])
                            S.op(ACT, lambda: nc.scalar.activation(sl[:, :], cb[:, :], AF.Silu),
                                 R=[("CB", kab)], W=[("SL", kab)])
                            S.op(DVE, lambda: nc.vector.tensor_tensor(Zt[:, j, :], sl[:, :], psum[bu][:, :], ALU.mult),
                                 R=[("SL", kab), ("ps", bu)], W=[("Z", t % 2, j)])
                        if t >= 1:
                            out_proj(t - 1)
                    out_proj(NT - 1)
                S.barrier()

        def conv_module(j, li):
            with contextlib.ExitStack() as es:
                with contextlib.ExitStack() as es2:
                    def asb2(name, shape, dt):
                        return es2.enter_context(nc.sbuf_tensor(uname(name), shape, dt))
                    G = asb2("c_g", [P, KC, TX], BF16)
                    SQ = [asb2(f"c_sq{k}", [P, TW], BF16) for k in range(2)]
                    RS = [asb2(f"c_rs{k}", [P, TW], F32) for k in range(2)]
                    SG = [asb2(f"c_sg{k}", [P, TW], F32) for k in range(2)]
                    DG = [asb2(f"c_dg{k}", [P, CW, P], BF16) for k in range(2)]
                    rmsnorm(("nmix", li), list(range(0, NT)) + [-1], (SQ, RS))
                    ksg = 0
                    for half in range(2):
                        sa, sg = ws.acquire(2)
                        A, Gw = slots[sa], slots[sg]
                        for ccl in range(4):
                            cch = half * 4 + ccl
                            for t in list(range(0, NT)) + [-1]:
                                n = HALO if t < 0 else TW
                                ba, bg = bank("A"), bank("B")
                                for kc in range(KC):
                                    mm(psum[ba][:, 0:n], A[:, kc, ccl * P:(ccl + 1) * P], Hn[:, kc, xc(t)],
                                       kc == 0, kc == KC - 1, R=[("W", sa), ("Hn", kc, t)], W=[("ps", ba)])
                                for kc in range(KC):
                                    mm(psum[bg][:, 0:n], Gw[:, kc, ccl * P:(ccl + 1) * P], Hn[:, kc, xc(t)],
                                       kc == 0, kc == KC - 1, R=[("W", sg), ("Hn", kc, t)], W=[("ps", bg)])
                                sgt = SG[ksg % 2]
                                ks = ksg % 2
                                ksg += 1
                                S.op(ACT, lambda: nc.scalar.activation(sgt[:, 0:n], psum[bg][:, 0:n], AF.Sigmoid),
                                     R=[("ps", bg)], W=[("SG", ks)])
                                S.op(DVE, lambda: nc.vector.tensor_tensor(G[:, cch, xc(t)], psum[ba][:, 0:n],
                                                                          sgt[:, 0:n], ALU.mult),
                                     R=[("ps", ba), ("SG", ks)], W=[("G", cch, t)])
                    def build_dg(cch):
                        dg = DG[cch % 2]
                        for tap in range(CW):
                            wv = vcol(("cdw", j), tap * KC + cch)
                            if tap % 2 == 0:
                                S.op(DVE, lambda: nc.vector.tensor_scalar(dg[:, tap, :], IDENT[:, :], wv, None,
                                                                          op0=ALU.mult),
                                     R=["IDENT", "VEC"], W=[("DG", cch % 2, tap)])
                            else:
                                S.op(ACT, lambda: nc.scalar.activation(dg[:, tap, :], IDENT[:, :], AF.Identity, scale=wv),
                                     R=["IDENT", "VEC"], W=[("DG", cch % 2, tap)])

                    build_dg(0)
                    for cch in range(KC):
                        dg = DG[cch % 2]
                        for ti, t in enumerate(list(range(1, NT)) + [0]):
                            if ti == 1 and cch + 1 < KC:
                                build_dg(cch + 1)
                            bc = bank("C")
                            base = HALO + t * TW - (CW - 1)
                            for tap in range(CW):
                                mm(psum[bc][:, :], dg[:, tap, :], G[:, cch, base + tap:base + tap + TW],
                                   tap == 0, tap == CW - 1,
                                   R=[("DG", cch % 2, tap), ("G", cch, t), ("G", cch, t - 1)], W=[("ps", bc)])
                            S.op(ACT, lambda: nc.scalar.activation(Hn[:, cch, xc(t)], psum[bc][:, :], AF.Identity,
                                                                   bias=vcol(("cdb", j), cch)),
                                 R=[("ps", bc), "VEC"], W=[("Hn", cch, t)])
                S.barrier()
                with contextlib.ExitStack() as es3:
                    def asb3(name, shape, dt):
                        return es3.enter_context(nc.sbuf_tensor(uname(name), shape, dt))
                    SQ = [asb3(f"c3_sq{k}", [P, TW], BF16) for k in range(2)]
                    MU = asb3("c3_mu", [P, TW], F32)
                    M2 = asb3("c3_m2", [P, TW], F32)
                    RSTD = asb3("c3_rstd", [P, TW], F32)
                    MR = asb3("c3_mr", [P, TW], F32)
                    T1 = [asb3(f"c3_t1{k}", [P, TW], F32) for k in range(2)]
                    Y = [asb3(f"c3_y{k}", [P, KC, TW], BF16) for k in range(2)]
                    so0, so1 = ws.acquire(2)

                    def stats_norm(t):
                        b1, b2 = bank("A"), bank("B")
                        for c in range(KC):
                            q = SQ[c % 2]
                            S.op(ACT, lambda: nc.scalar.activation(q[:, :], Hn[:, c, xc(t)], AF.Square),
                                 R=[("Hn", c, t)], W=[("SQ3", c % 2)])
                            mm(psum[b1][:, :], ONES[:, :], Hn[:, c, xc(t)], c == 0, c == KC - 1,
                               R=[("Hn", c, t), "ONES"], W=[("ps", b1)])
                            mm(psum[b2][:, :], ONES[:, :], q[:, :], c == 0, c == KC - 1,
                               R=[("SQ3", c % 2), "ONES"], W=[("ps", b2)], inc=True)
                        S.op(DVE, lambda: nc.vector.tensor_scalar(MU[:, :], psum[b1][:, :], 1.0 / D, None, op0=ALU.mult),
                             R=[("ps", b1)], W=["MU"])
                        S.op(DVE, lambda: nc.vector.tensor_tensor(M2[:, :], MU[:, :], MU[:, :], ALU.mult),
                             R=["MU"], W=["M2"])
                        S.op(DVE, lambda: nc.vector.scalar_tensor_tensor(M2[:, :], psum[b2][:, :], 1.0 / D, M2[:, :],
                                                                         op0=ALU.mult, op1=ALU.subtract),
                             R=[("ps", b2), "M2"], W=["M2"])
                        S.op(ACT, lambda: nc.scalar.activation(RSTD[:, :], M2[:, :], AF.Ln, bias=EPSV[:, 0:1]),
                             R=["M2", "EPSV"], W=["RSTD"])
                        S.op(ACT, lambda: nc.scalar.activation(RSTD[:, :], RSTD[:, :], AF.Exp, scale=-0.5),
                             R=["RSTD"], W=["RSTD"])
                        S.op(DVE, lambda: nc.vector.tensor_tensor(MR[:, :], MU[:, :], RSTD[:, :], ALU.mult),
                             R=["MU", "RSTD"], W=["MR"])
                        Yt = Y[t % 2]
                        for c in range(KC):
                            t1 = T1[c % 2]
                            S.op(DVE, lambda: nc.vector.tensor_tensor(t1[:, :], Hn[:, c, xc(t)], RSTD[:, :], ALU.mult),
                                 R=[("Hn", c, t), "RSTD"], W=[("T1", c % 2)])
                            S.op(DVE, lambda: nc.vector.tensor_tensor(t1[:, :], t1[:, :], MR[:, :], ALU.subtract),
                                 R=[("T1", c % 2), "MR"], W=[("T1", c % 2)])
                            S.op(ACT, lambda: nc.scalar.activation(Yt[:, c, :], t1[:, :], AF.Silu,
                                                                   bias=vcol(("clb", j), c), scale=vcol(("clg", j), c)),
                                 R=[("T1", c % 2), "VEC"], W=[("Y", t % 2, c)])

                    def out3(t):
                        Yt = Y[t % 2]
                        for oc in range(KC):
                            so = so0 if oc < 4 else so1
                            O = slots[so]
                            bo = bank("C")
                            for kc in range(KC):
                                mm(psum[bo][:, :], O[:, kc, (oc % 4) * P:(oc % 4 + 1) * P], Yt[:, kc, :],
                                   kc == 0, kc == KC - 1, R=[("W", so), ("Y", t % 2, kc)], W=[("ps", bo)])
                            residual_add(oc, t, bo)

                    stats_norm(0)
                    for t in range(NT):
                        if t + 1 < NT:
                            stats_norm(t + 1)
                        out3(t)
                S.barrier()

        def retention(j, li):
            gam = [1.0 - 2.0 ** (-5.0 - h) for h in range(NH)]
            with contextlib.ExitStack() as es:
                def asb(name, shape, dt):
                    return es.enter_context(nc.sbuf_tensor(uname(name), shape, dt))
                COS = asb("r_cos", [P, T], F32)
                SIN = asb("r_sin", [P, T], F32)
                CM = asb("r_cm", [P, NH * 2 * CH], F32)
                with contextlib.ExitStack() as es2:
                    def asb2(name, shape, dt):
                        return es2.enter_context(nc.sbuf_tensor(uname(name), shape, dt))
                    SQ = [asb2(f"r_sq{k}", [P, TW], BF16) for k in range(2)]
                    RS = [asb2(f"r_rs{k}", [P, TW], F32) for k in range(2)]
                    PI_ = asb2("r_posi", [P, T], I32)
                    ANG = asb2("r_ang", [P, T], F32)
                    rmsnorm(("nmix", li), range(0, NT), (SQ, RS))
                    S.dma(SP, ld, CM[:, :], consts_d, W=["CM"])
                    S.dma(SP, ld, PI_[:, :], pos.partition_broadcast(P), W=["PI"])
                    S.op(DVE, lambda: nc.vector.tensor_copy(ANG[:, :], PI_[:, :]), R=["PI"], W=["ANG"])
                    S.op(DVE, lambda: nc.vector.tensor_scalar(ANG[:, :], ANG[:, :], vcol("invf"), None, op0=ALU.mult),
                         R=["ANG", "VEC"], W=["ANG"])
                    two_pi = 2.0 * math.pi
                    C1 = float(np.float32(6.28125))
                    C2 = float(two_pi - 6.28125)
                    RR = asb2("r_rr", [P, T], F32)
                    MM = asb2("r_mm", [P, T], F32)
                    S.op(DVE, lambda: nc.vector.tensor_scalar(MM[:, :], ANG[:, :], 1.0 / two_pi, None, op0=ALU.mult),
                         R=["ANG"], W=["MM"])
                    S.op(DVE, lambda: nc.vector.tensor_copy(PI_[:, :], MM[:, :]), R=["MM"], W=["PI"])
                    S.op(DVE, lambda: nc.vector.tensor_copy(MM[:, :], PI_[:, :]), R=["PI"], W=["MM"])
                    S.op(DVE, lambda: nc.vector.scalar_tensor_tensor(RR[:, :], MM[:, :], -C1, ANG[:, :],
                                                                     op0=ALU.mult, op1=ALU.add),
                         R=["MM", "ANG"], W=["RR"])
                    S.op(DVE, lambda: nc.vector.scalar_tensor_tensor(RR[:, :], MM[:, :], -C2, RR[:, :],
                                                                     op0=ALU.mult, op1=ALU.add),
                         R=["MM", "RR"], W=["RR"])
                    S.op(DVE, lambda: nc.vector.tensor_scalar(MM[:, :], RR[:, :], math.pi, -two_pi,
                                                              op0=ALU.is_gt, op1=ALU.mult), R=["RR"], W=["MM"])
                    S.op(DVE, lambda: nc.vector.tensor_tensor(SIN[:, :], RR[:, :], MM[:, :], ALU.add),
                         R=["RR", "MM"], W=["SIN"])
                    S.op(DVE, lambda: nc.vector.tensor_scalar(COS[:, :], RR[:, :], 0.5 * math.pi, None, op0=ALU.add),
                         R=["RR"], W=["COS"])
                    S.op(DVE, lambda: nc.vector.tensor_scalar(MM[:, :], COS[:, :], math.pi, -two_pi,
                                                              op0=ALU.is_gt, op1=ALU.mult), R=["COS"], W=["MM"])
                    S.op(DVE, lambda: nc.vector.tensor_tensor(COS[:, :], COS[:, :], MM[:, :], ALU.add),
                         R=["COS", "MM"], W=["COS"])
                    for nm, tt in (("SIN", SIN), ("COS", COS)):
                        S.op(DVE, lambda: nc.vector.tensor_scalar(tt[:, :], tt[:, :], -math.pi, math.pi,
                                                                  op0=ALU.max, op1=ALU.min), R=[nm], W=[nm])
                    S.op(ACT, lambda: nc.scalar.activation(SIN[:, :], SIN[:, :], AF.Sin), R=["SIN"], W=["SIN"])
                    S.op(ACT, lambda: nc.scalar.activation(COS[:, :], COS[:, :], AF.Sin), R=["COS"], W=["COS"])
                S.barrier()
                KV = asb("r_kv", [P, 4096], BF16)
                KT = KV[:, 0:1024].rearrange("p (a b) -> p a b", b=TW)
                KTM = KV[:, 1024:2048].rearrange("p (a b) -> p a b", b=DK)
                V = KV[:, 2048:4096].rearrange("p (a b) -> p a b", b=DV)
                KV_CELLS = [("KT", 0), ("KT", 1), "KTM"] + [("V", cl) for cl in range(4)]
                QT2 = [asb(f"r_qt{k}", [P, 2, TW], BF16) for k in range(2)]
                KTG = asb("r_ktg", [P, 4, DK], BF16)
                SGT2 = [asb(f"r_sg{k}", [P, 4, TW], BF16) for k in range(2)]
                YT = asb("r_yt", [P, 4, TW], BF16)
                RA = asb("r_ra", [P, TW], F32)
                RB = asb("r_rb", [P, TW], F32)
                RC = asb("r_rc", [P, TW], F32)
                ST = asb("r_st", [P, CH], BF16)
                SS = asb("r_ss", [P, 2, DV], F32)
                SBF = asb("r_sbf", [P, 2, DV], BF16)
                ON = asb("r_on", [P, DV], BF16)
                BST = asb("r_bst", [P, 8], F32)
                BMV = asb("r_bmv", [P, 4], F32)

                def maskT(h):
                    return CM[:, (2 * h) * CH:(2 * h + 1) * CH]

                def qdec(h):
                    return CM[:, (2 * h + 1) * CH:(2 * h + 2) * CH]

                def rot(ps1, ps2, t, out, dec):
                    cs, sn = COS[:, t * TW:(t + 1) * TW], SIN[:, t * TW:(t + 1) * TW]
                    R0 = [("ps", ps1), ("ps", ps2), "COS", "SIN"]
                    for half in range(2):
                        pa, pb = (ps1, ps2) if half == 0 else (ps2, ps1)
                        S.op(DVE, lambda: nc.vector.tensor_tensor(RA[:, :], psum[pa][:, :], cs, ALU.mult),
                             R=R0, W=["RA"])
                        S.op(DVE, lambda: nc.vector.tensor_tensor(RB[:, :], psum[pb][:, :], sn, ALU.mult),
                             R=R0, W=["RB"])
                        op = ALU.subtract if half == 0 else ALU.add
                        if dec is None:
                            S.op(DVE, lambda: nc.vector.tensor_tensor(out[0][:, half, :], RA[:, :], RB[:, :], op),
                                 R=["RA", "RB"], W=[(out[1], half)])
                        else:
                            S.op(DVE, lambda: nc.vector.tensor_tensor(RC[:, :], RA[:, :], RB[:, :], op),
                                 R=["RA", "RB"], W=["RC"])
                            for cl in range(4):
                                S.op(POOL, lambda: nc.gpsimd.tensor_tensor(out[0][:, half, cl * CH:(cl + 1) * CH],
                                                                           RC[:, cl * CH:(cl + 1) * CH], dec, ALU.mult),
                                     R=["RC", "CM"], W=[(out[1], half)])

                def k_and_v(h, t, sqk, sv, want_g, kpool="A", vpool="C"):
                    k_proj(h, t, sqk, kpool)
                    v_part(h, t, sv, vpool)
                    k_fin(h, t, want_g)

                def k_proj(h, t, sqk, kpool):
                    QK = slots[sqk]
                    b1, b2 = bank(kpool), bank(kpool)
                    for half, b in ((0, b1), (1, b2)):
                        for kc in range(KC):
                            mm(psum[b][:, :], QK[:, kc, DK + half * P:DK + (half + 1) * P], Hn[:, kc, xc(t)],
                               kc == 0, kc == KC - 1, R=[("W", sqk), ("Hn", kc, t)], W=[("ps", b)])
                    rot(b1, b2, t, (KT, "KT"), None)

                def k_fin(h, t, want_g):
                    for cl in range(4):
                        for dch in range(2):
                            S.op(PE, lambda: nc.tensor.transpose(psb[:, (cl * 2 + dch) * P:(cl * 2 + dch + 1) * P],
                                                                 KT[:, dch, cl * CH:(cl + 1) * CH], IDENT[:, :]),
                                 R=[("KT", dch), "IDENT"], W=["psb"], inc=(cl == 3 and dch == 1))
                    S.op(ACT, lambda: nc.scalar.activation(KV[:, 1024:2048], psb[:, :], AF.Identity,
                                                           scale=vcol(("kdec", h))),
                         R=["psb", "VEC"], W=["KTM"])
                    if want_g:
                        for cl in range(4):
                            S.op(ACT, lambda: nc.scalar.activation(KTG[:, cl, :], psb[:, cl * DK:(cl + 1) * DK], AF.Identity,
                                                                   scale=vcol(("gd", h), t * 4 + cl)),
                                 R=["psb", "VEC"], W=[("KTG", cl)])

                def v_part(h, t, sv, vpool):
                    Vw = slots[sv]
                    for cl in range(4):
                        bv = bank(vpool)
                        for kc in range(KC):
                            mm(psum[bv][:, :], Hn[:, kc, HALO + t * TW + cl * CH:HALO + t * TW + (cl + 1) * CH],
                               Vw[:, kc, :], kc == 0, kc == KC - 1, R=[("W", sv), ("Hn", kc, t)], W=[("ps", bv)])
                        S.op(ACT, lambda: nc.scalar.copy(V[:, cl, :], psum[bv][:, :]), R=[("ps", bv)], W=[("V", cl)])

                for h in range(NH):
                    sqk, sv = ws.acquire(2)
                    bs = [2, 3]
                    for t in range(NT):
                        k_and_v(h, t, sqk, sv, True)
                        for cl in range(4):
                            first = (t == 0 and cl == 0)
                            last = (t == NT - 1 and cl == 3)
                            for dch in range(2):
                                mm(psum[bs[dch]][:, :], KTG[:, cl, dch * P:(dch + 1) * P], V[:, cl, :], first, last,
                                   R=[("KTG", cl), ("V", cl)], W=[("ps", bs[dch])])
                        S.dma(SP, kvst, kvs[j][h][t], KV[:, :], R=KV_CELLS, W=[("kvs", j, h, t)])
                    for dch in range(2):
                        S.op(ACT, lambda: nc.scalar.copy(SS[:, dch, :], psum[bs[dch]][:, :]),
                             R=[("ps", bs[dch])], W=[("SS", dch)])
                    S.dma(SP, xch, msgS[j][h], SS[:].rearrange("p c e -> p (c e)"), R=[("SS", 0), ("SS", 1)],
                          W=[("msgS", j, h)])
                    S.collective(cc, [msgS[j][h]], [gathS[j][h]], R=[("msgS", j, h)], W=[("gathS", j, h)])

                seq = [(hh, tt) for hh in range(NH) for tt in range(NT)]

                def load_kv(idx):
                    if idx < len(seq):
                        hh, tt = seq[idx]
                        S.dma(SP, kvld, KV[:, :], kvs[j][hh][tt], R=[("kvs", j, hh, tt)], W=KV_CELLS)

                load_kv(0)
                for h in range(NH):
                    sq, sgw, so = ws.acquire(3)
                    QW, Gw = slots[sq], slots[sgw]
                    O = slots[so][:].rearrange("p a b -> p (a b)").rearrange("p (r n) -> p r n", n=D)
                    gC = gam[h] ** CH
                    S.dma(SP, xch, SS[:].rearrange("p c e -> p (c e)"), gathS[j][h][0:P, :], R=[("gathS", j, h)],
                          W=[("SS", 0), ("SS", 1)])
                    S.op(DVE, lambda: nc.vector.tensor_scalar(SS[:].rearrange("p c e -> p (c e)"),
                                                              SS[:].rearrange("p c e -> p (c e)"),
                                                              vcol("flag"), None, op0=ALU.mult),
                         R=[("SS", 0), ("SS", 1), "VEC"], W=[("SS", 0), ("SS", 1)])
                    for dch in range(2):
                        S.op(ACT, lambda: nc.scalar.copy(SBF[:, dch, :], SS[:, dch, :]),
                             R=[("SS", dch)], W=[("SBF", dch)])

                    def Pg(t, echs):
                        SGT = SGT2[t % 2]
                        for ech in echs:
                            bg = bank("C")
                            for kc in range(KC):
                                mm(psum[bg][:, :], Gw[:, kc, ech * P:(ech + 1) * P], Hn[:, kc, xc(t)],
                                   kc == 0, kc == KC - 1, R=[("W", sgw), ("Hn", kc, t)], W=[("ps", bg)])
                            S.op(ACT, lambda: nc.scalar.activation(SGT[:, ech, :], psum[bg][:, :], AF.Silu),
                                 R=[("ps", bg)], W=[("SGT", t % 2, ech)])

                    def Pq(t):
                        bq1, bq2 = bank("C"), bank("C")
                        for half, b in ((0, bq1), (1, bq2)):
                            for kc in range(KC):
                                mm(psum[b][:, :], QW[:, kc, half * P:(half + 1) * P], Hn[:, kc, xc(t)],
                                   kc == 0, kc == KC - 1, R=[("W", sq), ("Hn", kc, t)], W=[("ps", b)])
                        rot(bq1, bq2, t, (QT2[t % 2], ("QT", t % 2)), None)

                    Pq(0)
                    Pg(0, range(4))
                    for t in range(NT):
                        SGT = SGT2[t % 2]
                        sgpar = t % 2
                        QT = QT2[t % 2]
                        qpar = t % 2
                        def stageA(cl):
                            cs_ = slice(cl * CH, (cl + 1) * CH)
                            bsc = bank("D")
                            for dch in range(2):
                                mm(psum[bsc][:, 0:CH], KT[:, dch, cs_], QT[:, dch, cs_], dch == 0, dch == 1,
                                   R=[("KT", dch), (("QT", qpar), dch)], W=[("ps", bsc)])
                            for dch in range(2):
                                mm(psB[:, dch, :], KTM[:, cl, dch * P:(dch + 1) * P], V[:, cl, :], True, True,
                                   R=["KTM", ("V", cl)], W=[("ps", 2 + dch)])
                            S.op(DVE, lambda: nc.vector.tensor_tensor(ST[:, :], psum[bsc][:, 0:CH], maskT(h), ALU.mult),
                                 R=[("ps", bsc), "CM"], W=["ST"])
                            bo = bank("A")
                            mm(psum[bo][:, :], ST[:, :], V[:, cl, :], True, False, R=["ST", ("V", cl)], W=[("ps", bo)])
                            for dch in range(2):
                                mm(psum[bo][:, :], QT[:, dch, cs_], SBF[:, dch, :], False, dch == 1,
                                   R=[(("QT", qpar), dch), ("SBF", dch)], W=[("ps", bo)])
                            S.op(DVE, lambda: nc.vector.scalar_tensor_tensor(SS[:, :, :], SS[:, :, :], float(gC),
                                                                             psB[:, :, :],
                                                                             op0=ALU.mult, op1=ALU.add),
                                 R=[("ps", 2), ("ps", 3), ("SS", 0), ("SS", 1)], W=[("SS", 0), ("SS", 1)])
                            S.op(ACT, lambda: nc.scalar.copy(SBF[:, :, :], SS[:, :, :]),
                                 R=[("SS", 0), ("SS", 1)], W=[("SBF", 0), ("SBF", 1)])
                            return bo

                        def stageB(cl, bo):
                            cs_ = slice(cl * CH, (cl + 1) * CH)
                            S.op(DVE, lambda: nc.vector.bn_stats(BST[:, 0:6], psum[bo][:, :]), R=[("ps", bo)], W=["BST"])
                            S.op(DVE, lambda: nc.vector.bn_aggr(BMV[:, 0:2], BST[:, 0:6]), R=["BST"], W=["BMV"])
                            S.op(ACT, lambda: nc.scalar.activation(BMV[:, 2:3], BMV[:, 1:2], AF.Sqrt, bias=vcol(("epsn", h))),
                                 R=["BMV", "VEC"], W=["BMV"])
                            S.op(DVE, lambda: nc.vector.reciprocal(BMV[:, 2:3], BMV[:, 2:3]), R=["BMV"], W=["BMV"])
                            S.op(DVE, lambda: nc.vector.tensor_scalar(ON[:, :], psum[bo][:, :], BMV[:, 0:1], BMV[:, 2:3],
                                                                      op0=ALU.subtract, op1=ALU.mult),
                                 R=[("ps", bo), "BMV"], W=["ON"])
                            for ech in range(4):
                                S.op(PE, lambda: nc.tensor.transpose(psb[:, ech * P:(ech + 1) * P],
                                                                     ON[:, ech * P:(ech + 1) * P], IDENT[:, :]),
                                     R=["ON", "IDENT"], W=["psb"], inc=(ech == 3))
                            for ech in range(4):
                                S.op(DVE, lambda: nc.vector.scalar_tensor_tensor(
                                    YT[:, ech, cs_], psb[:, ech * P:(ech + 1) * P], vcol(("gng", j), h * 4 + ech),
                                    SGT[:, ech, cs_], op0=ALU.mult, op1=ALU.mult),
                                    R=["psb", "VEC", ("SGT", sgpar, ech)], W=[("YT", ech)])

                        nxt = t + 1 < NT
                        bos = {0: stageA(0)}
                        if nxt:
                            Pg(t + 1, (0, 1))
                        bos[1] = stageA(1)
                        if nxt:
                            Pg(t + 1, (2, 3))
                        stageB(0, bos[0])
                        bos[2] = stageA(2)
                        stageB(1, bos[1])
                        bos[3] = stageA(3)
                        load_kv(h * NT + t + 1)
                        if nxt:
                            Pq(t + 1)
                        stageB(2, bos[2])
                        stageB(3, bos[3])
                        for oc in range(KC):
                            bo2 = bank("C")
                            for ech in range(4):
                                mm(psum[bo2][:, :], O[:, ech, oc * P:(oc + 1) * P], YT[:, ech, :], ech == 0, ech == 3,
                                   R=[("W", so), ("YT", ech)], W=[("ps", bo2)])
                            residual_add(oc, t, bo2)
                S.barrier()

        XT = sb("XT", [P, KC, HALO], F32)
        first = True
        for li in cfg.layers:
            if li % 2 == 0:
                if not first:
                    exchange_halo(XT)
                conv_module(li // 2, li)
            else:
                retention(li // 2, li)
            exchange_halo(XT)
            ffn(li)
            first = False

        if cfg.final_norm:
            with contextlib.ExitStack() as es:
                SQ = [es.enter_context(nc.sbuf_tensor(uname(f"n_sq{k}"), [P, TW], BF16)) for k in range(2)]
                RSF = [es.enter_context(nc.sbuf_tensor(f"n_rs{k}", [P, TW], F32)) for k in range(2)]
                for t in range(NT):
                    RS = RSF[t % 2]
                    b = bank("D") if t % 2 == 0 else bank("C")
                    for c in range(KC):
                        q = SQ[c % 2]
                        S.op(ACT, lambda: nc.scalar.activation(q[:, :], X[:, c, xc(t)], AF.Square),
                             R=[("X", c, t)], W=[("SQ", c % 2)])
                        mm(psum[b][:, :], ONES[:, :], q[:, :], c == 0, c == KC - 1,
                           R=[("SQ", c % 2), "ONES"], W=[("ps", b)], inc=True)
                    S.op(ACT, lambda: nc.scalar.activation(RS[:, :], psum[b][:, :], AF.Ln, bias=EPSV[:, 0:1],
                                                           scale=1.0 / D), R=[("ps", b), "EPSV"], W=[("RSF", t % 2)])
                    S.op(ACT, lambda: nc.scalar.activation(RS[:, :], RS[:, :], AF.Exp, scale=-0.5), R=[("RSF", t % 2)], W=[("RSF", t % 2)])
                    for c in range(KC):
                        S.op(DVE, lambda: nc.vector.scalar_tensor_tensor(
                            X[:, c, xc(t)], X[:, c, xc(t)], vcol("nfin", c), RS[:, :], op0=ALU.mult, op1=ALU.mult),
                            R=[("X", c, t), ("RSF", t % 2), "VEC"], W=[("X", c, t)])
                    if cfg.final_norm:
                        S.dma(SP, st_out, outT3[:, :, t * TW:(t + 1) * TW], X[:, :, xc(t)],
                              R=[("X", c, t) for c in range(KC)], W=[("out", t)])
                S.barrier()
        if not cfg.final_norm:
            for t in range(NT):
                S.dma(SP, st_out, outT3[:, :, t * TW:(t + 1) * TW], X[:, :, xc(t)],
                      R=[("X", c, t) for c in range(KC)], W=[("out", t)])
        S.finish()
        assert ws.next == len(ws.plan), (ws.next, len(ws.plan))
    return nc


def pack_vecs(inp, T, odd):
    voff, NV = vec_layout()
    v = np.zeros((P, NV), np.float32)

    def put(name, arr, nchunk):
        v[:, voff[name]:voff[name] + nchunk] = np.asarray(arr, np.float32).reshape(nchunk, P).T
    for i in range(4):
        put(("nmix", i), inp["norm_mix_g"][i], KC)
        put(("nffn", i), inp["norm_ffn_g"][i], KC)
        for tap in range(3):
            v[:, voff[("fdw", i)] + tap * NFC: voff[("fdw", i)] + (tap + 1) * NFC] = \
                np.asarray(inp["ffn_dw_w"][i][tap], np.float32).reshape(NFC, P).T
        put(("fdb", i), inp["ffn_dw_b"][i], NFC)
    put("nfin", inp["final_g"], KC)
    for j in range(2):
        for tap in range(CW):
            v[:, voff[("cdw", j)] + tap * KC: voff[("cdw", j)] + (tap + 1) * KC] = \
                np.asarray(inp["conv_dw_w"][j][tap], np.float32).reshape(KC, P).T
        put(("cdb", j), inp["conv_dw_b"][j], KC)
        put(("clg", j), inp["conv_ln_g"][j], KC)
        put(("clb", j), inp["conv_ln_b"][j], KC)
        put(("gng", j), inp["ret_gn_g"][j], 16)
    v[:, voff["flag"]] = 1.0 if odd else 0.0
    jj = np.arange(P, dtype=np.float32)
    v[:, voff["invf"]] = (1.0 / (np.float32(10000.0) ** (np.arange(0, DK, 2, dtype=np.float32) / np.float32(DK)))).astype(np.float32)
    scale = DK ** -0.5
    m = np.arange(CH, dtype=np.float64)
    for h in range(NH):
        g = 1.0 - 2.0 ** (-5.0 - h)
        v[:, voff[("kdec", h)]] = (scale * g ** (CH - 1 - m)).astype(np.float32)
        v[:, voff[("qc", h)]] = (g ** (m + 1.0)).astype(np.float32)
        v[:, voff[("qc2", h)]] = (g ** (2.0 * (m + 1.0))).astype(np.float32)
        v[:, voff[("epsn", h)]] = (EPS / g ** (2.0 * (m + 1.0))).astype(np.float32)
        for c in range(T // CH):
            v[:, voff[("gd", h)] + c] = (scale * g ** (T - 1 - (c * CH + m))).astype(np.float32)
    return v


def make_consts():
    cm = np.zeros((P, NH * 2 * CH), np.float32)
    m = np.arange(CH, dtype=np.float64)[:, None]
    n = np.arange(CH, dtype=np.float64)[None, :]
    for h in range(NH):
        g = 1.0 - 2.0 ** (-5.0 - h)
        scale = DK ** -0.5
        cm[:, (2 * h) * CH:(2 * h + 1) * CH] = np.where(m <= n, scale * g ** (-(m + 1.0)), 0.0)
        cm[:, (2 * h + 1) * CH:(2 * h + 2) * CH] = np.broadcast_to(g ** (n + 1.0), (P, CH))
    return cm


_PROG_CACHE = {}


def run_layers(inputs, layers, final_norm, x_override=None):
    x = np.asarray(inputs["x"], np.float32) if x_override is None else x_override
    B, SEQ, _ = x.shape
    T = SEQ // 2
    key = (T, tuple(layers), final_norm)
    if key not in _PROG_CACHE:
        _PROG_CACHE[key] = build_program(Cfg(T, list(layers), final_norm))
    nc = _PROG_CACHE[key]
    posi = np.asarray(inputs["positions"], np.int32)
    shared = {k: np.ascontiguousarray(np.asarray(inputs[k], np.float32)) for k in
              ("conv_w_in", "conv_w_out", "ret_w_in", "ret_w_out", "ffn_w_in", "ffn_w_out")}
    cm = make_consts()
    ident = np.eye(P, dtype=np.float32)
    in_maps = []
    for core in range(8):
        b, s = core // 2, core % 2
        xT = np.zeros((D, HALO + T), np.float32)
        xT[:, HALO:] = x[b, s * T:(s + 1) * T, :].T
        if s == 1:
            xT[:, :HALO] = x[b, T - HALO:T, :].T
        m = dict(shared)
        m["xT"] = xT
        m["pos"] = np.ascontiguousarray(posi[b, s * T:(s + 1) * T].reshape(1, T))
        m["vecs"] = pack_vecs(inputs, T, s == 1)
        m["cmask"] = cm
        m["ident"] = ident
        in_maps.append(m)
    res = run_bass_kernel_spmd(nc, in_maps, core_ids=list(range(8)))
    out = np.zeros((B, SEQ, D), np.float32)
    for core in range(8):
        b, s = core // 2, core % 2
        out[b, s * T:(s + 1) * T, :] = res.results[core]["outT"].T
    return out


def kernel(**inputs):
    return run_layers(inputs, [0, 1, 2, 3], True)
```

```python
import contextlib
import math
import numpy as np
import concourse.bass as bass
import concourse.mybir as mybir
from concourse.bass_utils import run_bass_kernel_spmd

F32 = mybir.dt.float32
BF16 = mybir.dt.bfloat16
I32 = mybir.dt.int32
AF = mybir.ActivationFunctionType
ALU = mybir.AluOpType

P = 128
D = 1024
KC = 8
FF = 2816
NFC = 22
CW = 31
HALO = 32
NH = 4
DK = 256
DV = 512
CH = 128
EPS = 1e-6
TW = 512
NSLOT = 5
EPOCH = 12000
PAIRS = [[0, 1], [2, 3], [4, 5], [6, 7]]


class Eng:
    def __init__(self, sched, name, handle, unit=1, is_pe=False, is_dma=False):
        self.s = sched
        self.name = name
        self.h = handle
        self.unit = unit
        self.is_pe = is_pe
        self.is_dma = is_dma
        self.count = 0
        self.sems = []
        self.waited = {}
        self.max_waited = 0

    def sem_val(self, idx):
        e = (idx - 1) // EPOCH
        while len(self.sems) <= e:
            self.sems.append(self.s.new_sem(f"{self.name}_{len(self.sems)}"))
        return self.sems[e], (idx - e * EPOCH) * self.unit


class Cell:
    __slots__ = ("w", "r")

    def __init__(self):
        self.w = None
        self.r = {}


class Sched:
    def __init__(self, nc, stack):
        self.nc = nc
        self.stack = stack
        self.cells = {}
        self.nsem = 0
        self.pe = Eng(self, "pe", nc.tensor, is_pe=True)
        self.act = Eng(self, "act", nc.scalar)
        self.dve = Eng(self, "dve", nc.vector)
        self.pool = Eng(self, "pool", nc.gpsimd)
        self.sp = Eng(self, "sp", nc.sync)
        self.engines = [self.pe, self.act, self.dve, self.pool, self.sp]
        self.streams = []

    def new_sem(self, name):
        self.nsem += 1
        return self.stack.enter_context(self.nc.semaphore(name))

    def stream(self, name, unit=16):
        st = Eng(self, name, None, unit=unit, is_dma=True)
        self.streams.append(st)
        return st

    def cell(self, k):
        c = self.cells.get(k)
        if c is None:
            c = self.cells[k] = Cell()
        return c

    def _deps(self, R, W):
        deps = {}

        def add(e, idx):
            if deps.get(e, 0) < idx:
                deps[e] = idx
        for k in R:
            c = self.cell(k)
            if c.w is not None:
                add(*c.w)
        for k in W:
            c = self.cell(k)
            if c.w is not None:
                add(*c.w)
            for e, idx in c.r.values():
                add(e, idx)
        return deps

    def _wait(self, eng, deps):
        for e, idx in deps.items():
            if e is eng and eng.is_pe:
                continue
            if e.is_dma:
                idx = e.count
            if eng.waited.get(e.name, 0) >= idx:
                continue
            sem, val = e.sem_val(idx)
            eng.h.wait_ge(sem, val)
            eng.waited[e.name] = idx
            if e.is_dma and e.max_waited < idx:
                e.max_waited = idx

    def _wait_exact(self, eng, deps, own_stream):
        rest = {e: i for e, i in deps.items() if e is not own_stream}
        self._wait(eng, rest)
        if own_stream in deps:
            idx = deps[own_stream]
            if eng.waited.get(own_stream.name, 0) < idx:
                sem, val = own_stream.sem_val(idx)
                eng.h.wait_ge(sem, val)
                eng.waited[own_stream.name] = idx

    def _record(self, who, idx, R, W):
        for k in R:
            self.cell(k).r[who.name] = (who, idx)
        for k in W:
            c = self.cell(k)
            c.w = (who, idx)
            c.r = {}

    def op(self, eng, fn, R=(), W=(), inc=True):
        self._wait(eng, self._deps(R, W))
        inst = fn()
        if inc:
            eng.count += 1
            sem, val = eng.sem_val(eng.count)
            inst.then_inc(sem, 1)
            idx = eng.count
        else:
            idx = eng.count + 1
        self._record(eng, idx, R, W)
        return inst

    def dma(self, issuer, stream, out, in_, R=(), W=(), **kw):
        deps = self._deps(R, W)
        if stream.max_waited > 0:
            deps[stream] = max(deps.get(stream, 0), stream.max_waited)
        self._wait_exact(issuer, deps, stream)
        stream.count += 1
        sem, val = stream.sem_val(stream.count)
        issuer.h.dma_start(out=out, in_=in_, **kw).then_inc(sem, 16)
        self._record(stream, stream.count, R, W)

    def collective(self, stream, ins, outs, R=(), W=()):
        issuer = self.pool
        deps = self._deps(R, W)
        if stream.max_waited > 0:
            deps[stream] = max(deps.get(stream, 0), stream.max_waited)
        self._wait_exact(issuer, deps, stream)
        stream.count += 1
        sem, val = stream.sem_val(stream.count)
        issuer.h.collective_compute("AllGather", ALU.bypass, replica_groups=PAIRS,
                                    ins=ins, outs=outs).then_inc(sem, 1)
        self._record(stream, stream.count, R, W)

    def barrier(self):
        for eng in (self.pe, self.act, self.dve, self.pool, self.sp):
            deps = {}
            for e in self.engines + self.streams:
                if e.count > 0 and e is not eng and not e.name.startswith("w"):
                    deps[e] = e.count
            saved = eng.is_pe
            self._wait(eng, deps)

    def finish(self):
        for eng in (self.sp,):
            deps = {e: e.count for e in self.streams + self.engines if e.count > 0 and e is not eng}
            self._wait(eng, deps)


class WStream:
    def __init__(self, S, slots):
        self.S = S
        self.slots = slots
        self.plan = []
        self.issued = 0
        self.next = 0
        self.streams = [S.stream(f"w{i}") for i in range(len(slots))]

    def add(self, loads):
        self.plan.append(loads)

    def _issue_upto(self, limit):
        S = self.S
        while self.issued < min(limit, len(self.plan)):
            i = self.issued
            s = i % len(self.slots)
            for dst_fn, src in self.plan[i]:
                S.dma(S.pool, self.streams[s], dst_fn(self.slots[s]), src, R=(), W=[("W", s)])
            self.issued += 1

    def acquire(self, n):
        c = self.next
        self._issue_upto(c + len(self.slots))
        assert self.issued >= c + n, (self.issued, c, n)
        res = [((c + j) % len(self.slots)) for j in range(n)]
        self.next += n
        return res


def split_groups(n, g):
    out = []
    i = 0
    while i < n:
        m = min(g, n - i)
        out.append(list(range(i, i + m)))
        i += m
    return out


class Cfg:
    def __init__(self, T, layers, final_norm=True):
        self.T = T
        self.NT = T // TW
        self.NCH = T // CH
        self.layers = layers
        self.final_norm = final_norm


def vec_layout():
    off = {}
    n = 0

    def add(name, cnt):
        nonlocal n
        off[name] = n
        n += cnt
    for i in range(4):
        add(("nmix", i), KC)
        add(("nffn", i), KC)
        add(("fdw", i), 3 * NFC)
        add(("fdb", i), NFC)
    add("nfin", KC)
    for j in range(2):
        add(("cdw", j), CW * KC)
        add(("cdb", j), KC)
        add(("clg", j), KC)
        add(("clb", j), KC)
        add(("gng", j), 16)
    add("flag", 1)
    add("invf", 1)
    for h in range(NH):
        add(("kdec", h), 1)
        add(("gd", h), 16)
        add(("qc", h), 1)
        add(("qc2", h), 1)
        add(("epsn", h), 1)
    return off, n


def build_program(cfg):
    T, NT, NCH = cfg.T, cfg.NT, cfg.NCH
    TX = HALO + T
    nc = bass.Bass("TRN2", target_bir_lowering=False)
    voff, NV = vec_layout()

    def din(name, shape, dt=F32):
        return nc.dram_tensor(name, shape, dt, kind="ExternalInput").ap()

    xT = din("xT", [D, TX])
    pos = din("pos", [1, T], I32)
    vecs_d = din("vecs", [P, NV])
    consts_d = din("cmask", [P, NH * 2 * CH])
    ident_d = din("ident", [P, P])
    conv_w_in = din("conv_w_in", [2, D, 2 * D])
    conv_w_out = din("conv_w_out", [2, D, D])
    ret_w_in = din("ret_w_in", [2, D, 6144])
    ret_w_out = din("ret_w_out", [2, 2048, D])
    ffn_w_in = din("ffn_w_in", [4, D, 2 * FF])
    ffn_w_out = din("ffn_w_out", [4, FF, D])
    outT = nc.dram_tensor("outT", [D, T], F32, kind="ExternalOutput").ap()

    n_x_exch = 8
    msgX = [nc.dram_tensor(f"msgX{i}", [P, KC * HALO], F32, kind="Internal").ap() for i in range(n_x_exch)]
    gathX = [nc.dram_tensor(f"gathX{i}", [2 * P, KC * HALO], F32, kind="Internal").ap() for i in range(n_x_exch)]
    msgS = [[nc.dram_tensor(f"msgS{i}_{h}", [P, 2 * DV], F32, kind="Internal").ap() for h in range(NH)] for i in range(2)]
    gathS = [[nc.dram_tensor(f"gathS{i}_{h}", [2 * P, 2 * DV], F32, kind="Internal").ap() for h in range(NH)] for i in range(2)]

    kvs = [[[nc.dram_tensor(f"kvs{jj}_{h}_{t}", [P, 4096], BF16, kind="Internal").ap() for t in range(NT)]
            for h in range(NH)] for jj in range(2)]

    stack = contextlib.ExitStack()
    with stack:
        S = Sched(nc, stack)
        PE, ACT, DVE, POOL, SP = S.pe, S.act, S.dve, S.pool, S.sp

        def sb(name, shape, dt):
            return stack.enter_context(nc.sbuf_tensor(name, shape, dt))

        uniq = [0]

        def uname(name):
            uniq[0] += 1
            return f"{name}_u{uniq[0]}"

        X = sb("X", [P, KC, TX], F32)
        Hn = sb("Hn", [P, KC, TX], BF16)
        slots = [sb(f"wslot{i}", [P, KC, TW], BF16) for i in range(NSLOT)]
        VEC = sb("VEC", [P, NV], F32)
        IDENT = sb("IDENT", [P, P], BF16)
        ONES = sb("ONES", [P, P], BF16)
        EPSV = sb("EPSV", [P, 1], F32)
        ps_lo = [stack.enter_context(nc.psum_tensor(f"ps{i}", [P, TW], F32)) for i in range(2)]
        psB = stack.enter_context(nc.psum_tensor("psB", [P, 2, TW], F32))
        ps_hi = [stack.enter_context(nc.psum_tensor(f"ps{i}", [P, TW], F32)) for i in range(4, 7)]
        psum = [ps_lo[0][:, :], ps_lo[1][:, :], psB[:, 0, :], psB[:, 1, :]] + [t_[:, :] for t_ in ps_hi]
        psb = stack.enter_context(nc.psum_tensor("psb", [P, 2 * TW], BF16))
        ps_rr = {"A": [0, 1], "B": [2, 3], "C": [4, 5], "D": [6]}
        ps_ctr = {k: 0 for k in ps_rr}

        def bank(pool):
            lst = ps_rr[pool]
            b = lst[ps_ctr[pool] % len(lst)]
            ps_ctr[pool] += 1
            return b

        ws = WStream(S, slots)
        ld = S.stream("ld")
        ldp = S.stream("ldp")
        st_out = S.stream("st")
        xch = S.stream("xch")
        kvst = S.stream("kvst")
        kvld = S.stream("kvld")
        cc = S.stream("cc", unit=1)

        def xc(t, lo=0, hi=None):
            if t < 0:
                return slice(0, HALO)
            base = HALO + t * TW
            return slice(base + lo, base + (TW if hi is None else hi))

        def vcol(name, i=0):
            o = voff[name] + i
            return VEC[:, o:o + 1]

        def wsrc(w2d, r0, nr, c0, ncol):
            return w2d[r0 * P:(r0 + nr) * P, c0:c0 + ncol].rearrange("(r p) n -> p r n", p=P)

        ffn_groups = split_groups(NFC, 4)

        def plan_conv(j):
            w_in, w_out = conv_w_in[j], conv_w_out[j]
            for half in range(2):
                ws.add([(lambda s: s[:, :, :], wsrc(w_in, 0, KC, half * 512, 512))])
                ws.add([(lambda s: s[:, :, :], wsrc(w_in, 0, KC, D + half * 512, 512))])
            for half in range(2):
                ws.add([(lambda s: s[:, :, :], wsrc(w_out, 0, KC, half * 512, 512))])

        def plan_ffn(i):
            w_in, w_out = ffn_w_in[i], ffn_w_out[i]
            for g in ffn_groups:
                n = len(g) * P
                ws.add([(lambda s, n=n: s[:, :, 0:n], wsrc(w_in, 0, KC, g[0] * P, n))])
                ws.add([(lambda s, n=n: s[:, :, 0:n], wsrc(w_in, 0, KC, FF + g[0] * P, n))])
                ws.add([(lambda s, g=g: s[:].rearrange("p a b -> p (a b)")[:, 0:len(g) * D]
                         .rearrange("p (r n) -> p r n", n=D),
                         wsrc(w_out, g[0], len(g), 0, D))])

        def plan_ret(j):
            w_in, w_out = ret_w_in[j], ret_w_out[j]

            for h in range(NH):
                ws.add([(lambda s: s[:, :, DK:2 * DK], wsrc(w_in, 0, KC, D + h * DK, DK))])
                ws.add([(lambda s: s[:, :, :], wsrc(w_in, 0, KC, 2 * D + h * DV, DV))])
            for h in range(NH):
                ws.add([(lambda s: s[:, :, 0:DK], wsrc(w_in, 0, KC, h * DK, DK))])
                ws.add([(lambda s: s[:, :, :], wsrc(w_in, 0, KC, 4 * D + h * DV, DV))])
                ws.add([(lambda s: s[:].rearrange("p a b -> p (a b)").rearrange("p (r n) -> p r n", n=D),
                         wsrc(w_out, h * 4, 4, 0, D))])

        for li in cfg.layers:
            if li % 2 == 0:
                plan_conv(li // 2)
            else:
                plan_ret(li // 2)
            plan_ffn(li)

        xT3 = xT.rearrange("(c p) n -> p c n", p=P)
        outT3 = outT.rearrange("(c p) n -> p c n", p=P)
        for t in list(range(NT)) + [-1]:
            S.dma(SP, ld, X[:, :, xc(t)], xT3[:, :, xc(t)], W=[("X", c, t) for c in range(KC)])
        S.dma(SP, ld, VEC[:, :], vecs_d, W=["VEC"])
        S.dma(POOL, ldp, IDENT[:, :], ident_d, W=["IDENT"])
        S.op(DVE, lambda: nc.vector.memset(ONES[:, :], 1.0), W=["ONES"])
        S.op(DVE, lambda: nc.vector.memset(EPSV[:, :], float(EPS)), W=["EPSV"])

        def mm(out, lhsT, rhs, start, stop, R, W, inc=None):
            S.op(PE, lambda: nc.tensor.matmul(out, lhsT, rhs, start=start, stop=stop), R=R, W=W,
                 inc=(stop if inc is None else inc))

        def rmsnorm(gname, tiles, scr):
            SQ, RS2 = scr
            for ti, t in enumerate(tiles):
                n = HALO if t < 0 else TW
                RS = RS2[ti % 2]
                rk = ("RS", ti % 2)
                b = bank("D") if ti % 2 == 0 else bank("C")
                for c in range(KC):
                    q = SQ[c % len(SQ)]
                    S.op(ACT, lambda: nc.scalar.activation(q[:, 0:n], X[:, c, xc(t)], AF.Square),
                         R=[("X", c, t)], W=[("SQ", c % len(SQ))])
                    mm(psum[b][:, 0:n], ONES[:, :], q[:, 0:n], c == 0, c == KC - 1,
                       R=[("SQ", c % len(SQ)), "ONES"], W=[("ps", b)], inc=True)
                S.op(ACT, lambda: nc.scalar.activation(RS[:, 0:n], psum[b][:, 0:n], AF.Ln, bias=EPSV[:, 0:1],
                                                       scale=1.0 / D),
                     R=[("ps", b), "EPSV"], W=[rk])
                S.op(ACT, lambda: nc.scalar.activation(RS[:, 0:n], RS[:, 0:n], AF.Exp, scale=-0.5), R=[rk], W=[rk])
                for c in range(KC):
                    S.op(DVE, lambda: nc.vector.scalar_tensor_tensor(
                        Hn[:, c, xc(t)], X[:, c, xc(t)], vcol(gname, c), RS[:, 0:n],
                        op0=ALU.mult, op1=ALU.mult),
                        R=[("X", c, t), rk, "VEC"], W=[("Hn", c, t)])

        def residual_add(oc, t, b):
            S.op(DVE, lambda: nc.vector.tensor_tensor(X[:, oc, xc(t)], X[:, oc, xc(t)], psum[b][:, :], ALU.add),
                 R=[("ps", b), ("X", oc, t)], W=[("X", oc, t)])

        xcount = [0]

        def exchange_halo(XT):
            i = xcount[0]
            xcount[0] += 1
            tl = NT - 1
            for c in range(KC):
                S.op(ACT, lambda: nc.scalar.copy(XT[:, c, :], X[:, c, HALO + T - HALO:HALO + T]),
                     R=[("X", c, tl)], W=["XT"])
            S.dma(SP, xch, msgX[i], XT[:].rearrange("p c h -> p (c h)"), R=["XT"], W=[("msgX", i)])
            S.collective(cc, [msgX[i]], [gathX[i]], R=[("msgX", i)], W=[("gathX", i)])
            S.dma(SP, xch, XT[:].rearrange("p c h -> p (c h)"), gathX[i][0:P, :], R=[("gathX", i)], W=["XT"])
            for c in range(KC):
                S.op(DVE, lambda: nc.vector.tensor_scalar(X[:, c, 0:HALO], XT[:, c, :], vcol("flag"), None,
                                                          op0=ALU.mult),
                     R=["XT", "VEC"], W=[("X", c, -1)])

        def ffn(i):
            with contextlib.ExitStack() as es:
                def asb(name, shape, dt):
                    return es.enter_context(nc.sbuf_tensor(uname(name), shape, dt))
                SQ = [asb(f"f_sq{k}", [P, TW], BF16) for k in range(4)]
                RS = [asb(f"f_rs{k}", [P, TW], F32) for k in range(2)]
                AB = [asb(f"f_ab{k}", [P, TW + 2], F32) for k in range(2)]
                CB = [asb(f"f_cb{k}", [P, TW], F32) for k in range(2)]
                SL = [asb(f"f_sl{k}", [P, TW], F32) for k in range(2)]
                Z = [asb(f"f_z{k}", [P, 4, TW], BF16) for k in range(2)]
                TAIL = asb("f_tail", [P, NFC, 2], F32)
                rmsnorm(("nffn", i), list(range(0, NT)) + [-1], (SQ, RS))
                k_ab = 0
                for g in ffn_groups:
                    sa, su, so = ws.acquire(3)
                    A, U = slots[sa], slots[su]
                    O = slots[so][:].rearrange("p a b -> p (a b)")[:, 0:len(g) * D].rearrange("p (r n) -> p r n", n=D)
                    def out_proj(tt):
                        Zo = Z[tt % 2]
                        for oc in range(KC):
                            bo = bank("C")
                            for j in range(len(g)):
                                mm(psum[bo][:, :], O[:, j, oc * P:(oc + 1) * P], Zo[:, j, :],
                                   j == 0, j == len(g) - 1, R=[("W", so), ("Z", tt % 2, j)], W=[("ps", bo)])
                            residual_add(oc, tt, bo)

                    for t in range(NT):
                        Zt = Z[t % 2]
                        for j, fc in enumerate(g):
                            ba, bu = bank("A"), bank("B")
                            hn_r = [("Hn", kc, t) for kc in range(KC)]
                            for kc in range(KC):
                                mm(psum[ba][:, :], A[:, kc, j * P:(j + 1) * P], Hn[:, kc, xc(t)],
                                   kc == 0, kc == KC - 1, R=[("W", sa), ("Hn", kc, t)], W=[("ps", ba)])
                            for kc in range(KC):
                                mm(psum[bu][:, :], U[:, kc, j * P:(j + 1) * P], Hn[:, kc, xc(t)],
                                   kc == 0, kc == KC - 1, R=[("W", su), ("Hn", kc, t)], W=[("ps", bu)])
                            ab = AB[k_ab % 2]
                            cb = CB[k_ab % 2]
                            sl = SL[k_ab % 2]
                            kab = k_ab % 2
                            k_ab += 1
                            if t == 0:
                                bh = bank("D")
                                for kc in range(KC):
                                    mm(psum[bh][:, 0:2], A[:, kc, j * P:(j + 1) * P], Hn[:, kc, HALO - 2:HALO],
                                       kc == 0, kc == KC - 1, R=[("W", sa), ("Hn", kc, -1)], W=[("ps", bh)])
                                S.op(ACT, lambda: nc.scalar.copy(ab[:, 0:2], psum[bh][:, 0:2]),
                                     R=[("ps", bh)], W=[("AB", kab)])
                            else:
                                S.op(ACT, lambda: nc.scalar.copy(ab[:, 0:2], TAIL[:, fc, :]),
                                     R=[("TAIL", fc)], W=[("AB", kab)])
                            S.op(ACT, lambda: nc.scalar.copy(ab[:, 2:TW + 2], psum[ba][:, :]),
                                 R=[("ps", ba)], W=[("AB", kab)])
                            if t < NT - 1:
                                S.op(ACT, lambda: nc.scalar.copy(TAIL[:, fc, :], ab[:, TW:TW + 2]),
                                     R=[("AB", kab)], W=[("TAIL", fc)])
                            w0 = vcol(("fdw", i), 0 * NFC + fc)
                            w1 = vcol(("fdw", i), 1 * NFC + fc)
                            w2 = vcol(("fdw", i), 2 * NFC + fc)
                            bb = vcol(("fdb", i), fc)
                            S.op(DVE, lambda: nc.vector.tensor_scalar(cb[:, :], ab[:, 2:TW + 2], w2, bb,
                                                                      op0=ALU.mult, op1=ALU.add),
                                 R=[("AB", kab), "VEC"], W=[("CB", kab)])
                            S.op(DVE, lambda: nc.vector.scalar_tensor_tensor(cb[:, :], ab[:, 1:TW + 1], w1, cb[:, :],
                                                                             op0=ALU.mult, op1=ALU.add),
                                 R=[("AB", kab), ("CB", kab)], W=[("CB", kab)])
                            S.op(DVE, lambda: nc.vector.scalar_tensor_tensor(cb[:, :], ab[:, 0:TW], w0, cb[:, :],
                                                                             op0=ALU.mult, op1=ALU.add),
                                 R=[("AB", kab), ("CB", kab)], W=[("CB", kab)])
                            S.op(ACT, lambda: nc.scalar.activation(sl[:, :], cb[:, :], AF.Silu),
                                 R=[("CB", kab)], W=[("SL", kab)])
                            S.op(DVE, lambda: nc.vector.tensor_tensor(Zt[:, j, :], sl[:, :], psum[bu][:, :], ALU.mult),
                                 R=[("SL", kab), ("ps", bu)], W=[("Z", t % 2, j)])
                        if t >= 1:
                            out_proj(t - 1)
                    out_proj(NT - 1)
                S.barrier()

        def conv_module(j, li):
            with contextlib.ExitStack() as es:
                with contextlib.ExitStack() as es2:
                    def asb2(name, shape, dt):
                        return es2.enter_context(nc.sbuf_tensor(uname(name), shape, dt))
                    G = asb2("c_g", [P, KC, TX], BF16)
                    SQ = [asb2(f"c_sq{k}", [P, TW], BF16) for k in range(2)]
                    RS = [asb2(f"c_rs{k}", [P, TW], F32) for k in range(2)]
                    SG = [asb2(f"c_sg{k}", [P, TW], F32) for k in range(2)]
                    DG = [asb2(f"c_dg{k}", [P, CW, P], BF16) for k in range(2)]
                    rmsnorm(("nmix", li), list(range(0, NT)) + [-1], (SQ, RS))
                    ksg = 0
                    for half in range(2):
                        sa, sg = ws.acquire(2)
                        A, Gw = slots[sa], slots[sg]
                        for ccl in range(4):
                            cch = half * 4 + ccl
                            for t in list(range(0, NT)) + [-1]:
                                n = HALO if t < 0 else TW
                                ba, bg = bank("A"), bank("B")
                                for kc in range(KC):
                                    mm(psum[ba][:, 0:n], A[:, kc, ccl * P:(ccl + 1) * P], Hn[:, kc, xc(t)],
                                       kc == 0, kc == KC - 1, R=[("W", sa), ("Hn", kc, t)], W=[("ps", ba)])
                                for kc in range(KC):
                                    mm(psum[bg][:, 0:n], Gw[:, kc, ccl * P:(ccl + 1) * P], Hn[:, kc, xc(t)],
                                       kc == 0, kc == KC - 1, R=[("W", sg), ("Hn", kc, t)], W=[("ps", bg)])
                                sgt = SG[ksg % 2]
                                ks = ksg % 2
                                ksg += 1
                                S.op(ACT, lambda: nc.scalar.activation(sgt[:, 0:n], psum[bg][:, 0:n], AF.Sigmoid),
                                     R=[("ps", bg)], W=[("SG", ks)])
                                S.op(DVE, lambda: nc.vector.tensor_tensor(G[:, cch, xc(t)], psum[ba][:, 0:n],
                                                                          sgt[:, 0:n], ALU.mult),
                                     R=[("ps", ba), ("SG", ks)], W=[("G", cch, t)])
                    def build_dg(cch):
                        dg = DG[cch % 2]
                        for tap in range(CW):
                            wv = vcol(("cdw", j), tap * KC + cch)
                            if tap % 2 == 0:
                                S.op(DVE, lambda: nc.vector.tensor_scalar(dg[:, tap, :], IDENT[:, :], wv, None,
                                                                          op0=ALU.mult),
                                     R=["IDENT", "VEC"], W=[("DG", cch % 2, tap)])
                            else:
                                S.op(ACT, lambda: nc.scalar.activation(dg[:, tap, :], IDENT[:, :], AF.Identity, scale=wv),
                                     R=["IDENT", "VEC"], W=[("DG", cch % 2, tap)])

                    build_dg(0)
                    for cch in range(KC):
                        dg = DG[cch % 2]
                        for ti, t in enumerate(list(range(1, NT)) + [0]):
                            if ti == 1 and cch + 1 < KC:
                                build_dg(cch + 1)
                            bc = bank("C")
                            base = HALO + t * TW - (CW - 1)
                            for tap in range(CW):
                                mm(psum[bc][:, :], dg[:, tap, :], G[:, cch, base + tap:base + tap + TW],
                                   tap == 0, tap == CW - 1,
                                   R=[("DG", cch % 2, tap), ("G", cch, t), ("G", cch, t - 1)], W=[("ps", bc)])
                            S.op(ACT, lambda: nc.scalar.activation(Hn[:, cch, xc(t)], psum[bc][:, :], AF.Identity,
                                                                   bias=vcol(("cdb", j), cch)),
                                 R=[("ps", bc), "VEC"], W=[("Hn", cch, t)])
                S.barrier()
                with contextlib.ExitStack() as es3:
                    def asb3(name, shape, dt):
                        return es3.enter_context(nc.sbuf_tensor(uname(name), shape, dt))
                    SQ = [asb3(f"c3_sq{k}", [P, TW], BF16) for k in range(2)]
                    MU = asb3("c3_mu", [P, TW], F32)
                    M2 = asb3("c3_m2", [P, TW], F32)
                    RSTD = asb3("c3_rstd", [P, TW], F32)
                    MR = asb3("c3_mr", [P, TW], F32)
                    T1 = [asb3(f"c3_t1{k}", [P, TW], F32) for k in range(2)]
                    Y = [asb3(f"c3_y{k}", [P, KC, TW], BF16) for k in range(2)]
                    so0, so1 = ws.acquire(2)

                    def stats_norm(t):
                        b1, b2 = bank("A"), bank("B")
                        for c in range(KC):
                            q = SQ[c % 2]
                            S.op(ACT, lambda: nc.scalar.activation(q[:, :], Hn[:, c, xc(t)], AF.Square),
                                 R=[("Hn", c, t)], W=[("SQ3", c % 2)])
                            mm(psum[b1][:, :], ONES[:, :], Hn[:, c, xc(t)], c == 0, c == KC - 1,
                               R=[("Hn", c, t), "ONES"], W=[("ps", b1)])
                            mm(psum[b2][:, :], ONES[:, :], q[:, :], c == 0, c == KC - 1,
                               R=[("SQ3", c % 2), "ONES"], W=[("ps", b2)], inc=True)
                        S.op(DVE, lambda: nc.vector.tensor_scalar(MU[:, :], psum[b1][:, :], 1.0 / D, None, op0=ALU.mult),
                             R=[("ps", b1)], W=["MU"])
                        S.op(DVE, lambda: nc.vector.tensor_tensor(M2[:, :], MU[:, :], MU[:, :], ALU.mult),
                             R=["MU"], W=["M2"])
                        S.op(DVE, lambda: nc.vector.scalar_tensor_tensor(M2[:, :], psum[b2][:, :], 1.0 / D, M2[:, :],
                                                                         op0=ALU.mult, op1=ALU.subtract),
                             R=[("ps", b2), "M2"], W=["M2"])
                        S.op(ACT, lambda: nc.scalar.activation(RSTD[:, :], M2[:, :], AF.Ln, bias=EPSV[:, 0:1]),
                             R=["M2", "EPSV"], W=["RSTD"])
                        S.op(ACT, lambda: nc.scalar.activation(RSTD[:, :], RSTD[:, :], AF.Exp, scale=-0.5),
                             R=["RSTD"], W=["RSTD"])
                        S.op(DVE, lambda: nc.vector.tensor_tensor(MR[:, :], MU[:, :], RSTD[:, :], ALU.mult),
                             R=["MU", "RSTD"], W=["MR"])
                        Yt = Y[t % 2]
                        for c in range(KC):
                            t1 = T1[c % 2]
                            S.op(DVE, lambda: nc.vector.tensor_tensor(t1[:, :], Hn[:, c, xc(t)], RSTD[:, :], ALU.mult),
                                 R=[("Hn", c, t), "RSTD"], W=[("T1", c % 2)])
                            S.op(DVE, lambda: nc.vector.tensor_tensor(t1[:, :], t1[:, :], MR[:, :], ALU.subtract),
                                 R=[("T1", c % 2), "MR"], W=[("T1", c % 2)])
                            S.op(ACT, lambda: nc.scalar.activation(Yt[:, c, :], t1[:, :], AF.Silu,
                                                                   bias=vcol(("clb", j), c), scale=vcol(("clg", j), c)),
                                 R=[("T1", c % 2), "VEC"], W=[("Y", t % 2, c)])

                    def out3(t):
                        Yt = Y[t % 2]
                        for oc in range(KC):
                            so = so0 if oc < 4 else so1
                            O = slots[so]
                            bo = bank("C")
                            for kc in range(KC):
                                mm(psum[bo][:, :], O[:, kc, (oc % 4) * P:(oc % 4 + 1) * P], Yt[:, kc, :],
                                   kc == 0, kc == KC - 1, R=[("W", so), ("Y", t % 2, kc)], W=[("ps", bo)])
                            residual_add(oc, t, bo)

                    stats_norm(0)
                    for t in range(NT):
                        if t + 1 < NT:
                            stats_norm(t + 1)
                        out3(t)
                S.barrier()

        def retention(j, li):
            gam = [1.0 - 2.0 ** (-5.0 - h) for h in range(NH)]
            with contextlib.ExitStack() as es:
                def asb(name, shape, dt):
                    return es.enter_context(nc.sbuf_tensor(uname(name), shape, dt))
                COS = asb("r_cos", [P, T], F32)
                SIN = asb("r_sin", [P, T], F32)
                CM = asb("r_cm", [P, NH * 2 * CH], F32)
                with contextlib.ExitStack() as es2:
                    def asb2(name, shape, dt):
                        return es2.enter_context(nc.sbuf_tensor(uname(name), shape, dt))
                    SQ = [asb2(f"r_sq{k}", [P, TW], BF16) for k in range(4)]
                    RS = [asb2(f"r_rs{k}", [P, TW], F32) for k in range(2)]
                    PI_ = asb2("r_posi", [P, T], I32)
                    ANG = asb2("r_ang", [P, T], F32)
                    rmsnorm(("nmix", li), range(0, NT), (SQ, RS))
                    S.dma(SP, ld, CM[:, :], consts_d, W=["CM"])
                    S.dma(SP, ld, PI_[:, :], pos.partition_broadcast(P), W=["PI"])
                    S.op(DVE, lambda: nc.vector.tensor_copy(ANG[:, :], PI_[:, :]), R=["PI"], W=["ANG"])
                    S.op(DVE, lambda: nc.vector.tensor_scalar(ANG[:, :], ANG[:, :], vcol("invf"), None, op0=ALU.mult),
                         R=["ANG", "VEC"], W=["ANG"])
                    two_pi = 2.0 * math.pi
                    C1 = float(np.float32(6.28125))
                    C2 = float(two_pi - 6.28125)
                    RR = asb2("r_rr", [P, T], F32)
                    MM = asb2("r_mm", [P, T], F32)
                    S.op(DVE, lambda: nc.vector.tensor_scalar(MM[:, :], ANG[:, :], 1.0 / two_pi, None, op0=ALU.mult),
                         R=["ANG"], W=["MM"])
                    S.op(DVE, lambda: nc.vector.tensor_copy(PI_[:, :], MM[:, :]), R=["MM"], W=["PI"])
                    S.op(DVE, lambda: nc.vector.tensor_copy(MM[:, :], PI_[:, :]), R=["PI"], W=["MM"])
                    S.op(DVE, lambda: nc.vector.scalar_tensor_tensor(RR[:, :], MM[:, :], -C1, ANG[:, :],
                                                                     op0=ALU.mult, op1=ALU.add),
                         R=["MM", "ANG"], W=["RR"])
                    S.op(DVE, lambda: nc.vector.scalar_tensor_tensor(RR[:, :], MM[:, :], -C2, RR[:, :],
                                                                     op0=ALU.mult, op1=ALU.add),
                         R=["MM", "RR"], W=["RR"])
                    S.op(DVE, lambda: nc.vector.tensor_scalar(MM[:, :], RR[:, :], math.pi, -two_pi,
                                                              op0=ALU.is_gt, op1=ALU.mult), R=["RR"], W=["MM"])
                    S.op(DVE, lambda: nc.vector.tensor_tensor(SIN[:, :], RR[:, :], MM[:, :], ALU.add),
                         R=["RR", "MM"], W=["SIN"])
                    S.op(DVE, lambda: nc.vector.tensor_scalar(COS[:, :], RR[:, :], 0.5 * math.pi, None, op0=ALU.add),
                         R=["RR"], W=["COS"])
                    S.op(DVE, lambda: nc.vector.tensor_scalar(MM[:, :], COS[:, :], math.pi, -two_pi,
                                                              op0=ALU.is_gt, op1=ALU.mult), R=["COS"], W=["MM"])
                    S.op(DVE, lambda: nc.vector.tensor_tensor(COS[:, :], COS[:, :], MM[:, :], ALU.add),
                         R=["COS", "MM"], W=["COS"])
                    for nm, tt in (("SIN", SIN), ("COS", COS)):
                        S.op(DVE, lambda: nc.vector.tensor_scalar(tt[:, :], tt[:, :], -math.pi, math.pi,
                                                                  op0=ALU.max, op1=ALU.min), R=[nm], W=[nm])
                    S.op(ACT, lambda: nc.scalar.activation(SIN[:, :], SIN[:, :], AF.Sin), R=["SIN"], W=["SIN"])
                    S.op(ACT, lambda: nc.scalar.activation(COS[:, :], COS[:, :], AF.Sin), R=["COS"], W=["COS"])
                S.barrier()
                KV = asb("r_kv", [P, 4096], BF16)
                KT = KV[:, 0:1024].rearrange("p (a b) -> p a b", b=TW)
                KTM = KV[:, 1024:2048].rearrange("p (a b) -> p a b", b=DK)
                V = KV[:, 2048:4096].rearrange("p (a b) -> p a b", b=DV)
                KV_CELLS = [("KT", 0), ("KT", 1), "KTM"] + [("V", cl) for cl in range(4)]
                QT2 = [asb(f"r_qt{k}", [P, 2, TW], BF16) for k in range(2)]
                KTG = asb("r_ktg", [P, 4, DK], BF16)
                SGT2 = [asb(f"r_sg{k}", [P, 4, TW], BF16) for k in range(2)]
                YT = asb("r_yt", [P, 4, TW], BF16)
                RA = asb("r_ra", [P, TW], F32)
                RB = asb("r_rb", [P, TW], F32)
                RC = asb("r_rc", [P, TW], F32)
                ST = asb("r_st", [P, CH], BF16)
                SS = asb("r_ss", [P, 2, DV], F32)
                SBF = asb("r_sbf", [P, 2, DV], BF16)
                ON = asb("r_on", [P, DV], BF16)
                BST = asb("r_bst", [P, 8], F32)
                BMV = asb("r_bmv", [P, 4], F32)

                def maskT(h):
                    return CM[:, (2 * h) * CH:(2 * h + 1) * CH]

                def qdec(h):
                    return CM[:, (2 * h + 1) * CH:(2 * h + 2) * CH]

                def rot(ps1, ps2, t, out, dec):
                    cs, sn = COS[:, t * TW:(t + 1) * TW], SIN[:, t * TW:(t + 1) * TW]
                    R0 = [("ps", ps1), ("ps", ps2), "COS", "SIN"]
                    for half in range(2):
                        pa, pb = (ps1, ps2) if half == 0 else (ps2, ps1)
                        S.op(DVE, lambda: nc.vector.tensor_tensor(RA[:, :], psum[pa][:, :], cs, ALU.mult),
                             R=R0, W=["RA"])
                        S.op(DVE, lambda: nc.vector.tensor_tensor(RB[:, :], psum[pb][:, :], sn, ALU.mult),
                             R=R0, W=["RB"])
                        op = ALU.subtract if half == 0 else ALU.add
                        if dec is None:
                            S.op(DVE, lambda: nc.vector.tensor_tensor(out[0][:, half, :], RA[:, :], RB[:, :], op),
                                 R=["RA", "RB"], W=[(out[1], half)])
                        else:
                            S.op(DVE, lambda: nc.vector.tensor_tensor(RC[:, :], RA[:, :], RB[:, :], op),
                                 R=["RA", "RB"], W=["RC"])
                            for cl in range(4):
                                S.op(POOL, lambda: nc.gpsimd.tensor_tensor(out[0][:, half, cl * CH:(cl + 1) * CH],
                                                                           RC[:, cl * CH:(cl + 1) * CH], dec, ALU.mult),
                                     R=["RC", "CM"], W=[(out[1], half)])

                def k_and_v(h, t, sqk, sv, want_g, kpool="A", vpool="C"):
                    k_proj(h, t, sqk, kpool)
                    v_part(h, t, sv, vpool)
                    k_fin(h, t, want_g)

                def k_proj(h, t, sqk, kpool):
                    QK = slots[sqk]
                    b1, b2 = bank(kpool), bank(kpool)
                    for half, b in ((0, b1), (1, b2)):
                        for kc in range(KC):
                            mm(psum[b][:, :], QK[:, kc, DK + half * P:DK + (half + 1) * P], Hn[:, kc, xc(t)],
                               kc == 0, kc == KC - 1, R=[("W", sqk), ("Hn", kc, t)], W=[("ps", b)])
                    rot(b1, b2, t, (KT, "KT"), None)

                def k_fin(h, t, want_g):
                    for cl in range(4):
                        for dch in range(2):
                            S.op(PE, lambda: nc.tensor.transpose(psb[:, (cl * 2 + dch) * P:(cl * 2 + dch + 1) * P],
                                                                 KT[:, dch, cl * CH:(cl + 1) * CH], IDENT[:, :]),
                                 R=[("KT", dch), "IDENT"], W=["psb"], inc=(cl == 3 and dch == 1))
                    S.op(ACT, lambda: nc.scalar.activation(KV[:, 1024:2048], psb[:, :], AF.Identity,
                                                           scale=vcol(("kdec", h))),
                         R=["psb", "VEC"], W=["KTM"])
                    if want_g:
                        for cl in range(4):
                            S.op(ACT, lambda: nc.scalar.activation(KTG[:, cl, :], psb[:, cl * DK:(cl + 1) * DK], AF.Identity,
                                                                   scale=vcol(("gd", h), t * 4 + cl)),
                                 R=["psb", "VEC"], W=[("KTG", cl)])

                def v_part(h, t, sv, vpool):
                    Vw = slots[sv]
                    for cl in range(4):
                        bv = bank(vpool)
                        for kc in range(KC):
                            mm(psum[bv][:, :], Hn[:, kc, HALO + t * TW + cl * CH:HALO + t * TW + (cl + 1) * CH],
                               Vw[:, kc, :], kc == 0, kc == KC - 1, R=[("W", sv), ("Hn", kc, t)], W=[("ps", bv)])
                        S.op(ACT, lambda: nc.scalar.copy(V[:, cl, :], psum[bv][:, :]), R=[("ps", bv)], W=[("V", cl)])

                for h in range(NH):
                    sqk, sv = ws.acquire(2)
                    bs = [2, 3]
                    for t in range(NT):
                        k_and_v(h, t, sqk, sv, True)
                        for cl in range(4):
                            first = (t == 0 and cl == 0)
                            last = (t == NT - 1 and cl == 3)
                            for dch in range(2):
                                mm(psum[bs[dch]][:, :], KTG[:, cl, dch * P:(dch + 1) * P], V[:, cl, :], first, last,
                                   R=[("KTG", cl), ("V", cl)], W=[("ps", bs[dch])])
                        S.dma(SP, kvst, kvs[j][h][t], KV[:, :], R=KV_CELLS, W=[("kvs", j, h, t)])
                    for dch in range(2):
                        S.op(ACT, lambda: nc.scalar.copy(SS[:, dch, :], psum[bs[dch]][:, :]),
                             R=[("ps", bs[dch])], W=[("SS", dch)])
                    S.dma(SP, xch, msgS[j][h], SS[:].rearrange("p c e -> p (c e)"), R=[("SS", 0), ("SS", 1)],
                          W=[("msgS", j, h)])
                    S.collective(cc, [msgS[j][h]], [gathS[j][h]], R=[("msgS", j, h)], W=[("gathS", j, h)])

                seq = [(hh, tt) for hh in range(NH) for tt in range(NT)]

                def load_kv(idx):
                    if idx < len(seq):
                        hh, tt = seq[idx]
                        S.dma(SP, kvld, KV[:, :], kvs[j][hh][tt], R=[("kvs", j, hh, tt)], W=KV_CELLS)

                load_kv(0)
                for h in range(NH):
                    sq, sgw, so = ws.acquire(3)
                    QW, Gw = slots[sq], slots[sgw]
                    O = slots[so][:].rearrange("p a b -> p (a b)").rearrange("p (r n) -> p r n", n=D)
                    gC = gam[h] ** CH
                    S.dma(SP, xch, SS[:].rearrange("p c e -> p (c e)"), gathS[j][h][0:P, :], R=[("gathS", j, h)],
                          W=[("SS", 0), ("SS", 1)])
                    S.op(DVE, lambda: nc.vector.tensor_scalar(SS[:].rearrange("p c e -> p (c e)"),
                                                              SS[:].rearrange("p c e -> p (c e)"),
                                                              vcol("flag"), None, op0=ALU.mult),
                         R=[("SS", 0), ("SS", 1), "VEC"], W=[("SS", 0), ("SS", 1)])
                    for dch in range(2):
                        S.op(ACT, lambda: nc.scalar.copy(SBF[:, dch, :], SS[:, dch, :]),
                             R=[("SS", dch)], W=[("SBF", dch)])

                    def Pg(t, echs):
                        SGT = SGT2[t % 2]
                        for ech in echs:
                            bg = bank("C")
                            for kc in range(KC):
                                mm(psum[bg][:, :], Gw[:, kc, ech * P:(ech + 1) * P], Hn[:, kc, xc(t)],
                                   kc == 0, kc == KC - 1, R=[("W", sgw), ("Hn", kc, t)], W=[("ps", bg)])
                            S.op(ACT, lambda: nc.scalar.activation(SGT[:, ech, :], psum[bg][:, :], AF.Silu),
                                 R=[("ps", bg)], W=[("SGT", t % 2, ech)])

                    def Pq(t):
                        bq1, bq2 = bank("C"), bank("C")
                        for half, b in ((0, bq1), (1, bq2)):
                            for kc in range(KC):
                                mm(psum[b][:, :], QW[:, kc, half * P:(half + 1) * P], Hn[:, kc, xc(t)],
                                   kc == 0, kc == KC - 1, R=[("W", sq), ("Hn", kc, t)], W=[("ps", b)])
                        rot(bq1, bq2, t, (QT2[t % 2], ("QT", t % 2)), None)

                    Pq(0)
                    Pg(0, range(4))
                    for t in range(NT):
                        SGT = SGT2[t % 2]
                        sgpar = t % 2
                        QT = QT2[t % 2]
                        qpar = t % 2
                        def stageA(cl):
                            cs_ = slice(cl * CH, (cl + 1) * CH)
                            bsc = bank("D")
                            for dch in range(2):
                                mm(psum[bsc][:, 0:CH], KT[:, dch, cs_], QT[:, dch, cs_], dch == 0, dch == 1,
                                   R=[("KT", dch), (("QT", qpar), dch)], W=[("ps", bsc)])
                            for dch in range(2):
                                mm(psB[:, dch, :], KTM[:, cl, dch * P:(dch + 1) * P], V[:, cl, :], True, True,
                                   R=["KTM", ("V", cl)], W=[("ps", 2 + dch)])
                            S.op(DVE, lambda: nc.vector.tensor_tensor(ST[:, :], psum[bsc][:, 0:CH], maskT(h), ALU.mult),
                                 R=[("ps", bsc), "CM"], W=["ST"])
                            bo = bank("A")
                            mm(psum[bo][:, :], ST[:, :], V[:, cl, :], True, False, R=["ST", ("V", cl)], W=[("ps", bo)])
                            for dch in range(2):
                                mm(psum[bo][:, :], QT[:, dch, cs_], SBF[:, dch, :], False, dch == 1,
                                   R=[(("QT", qpar), dch), ("SBF", dch)], W=[("ps", bo)])
                            S.op(DVE, lambda: nc.vector.scalar_tensor_tensor(SS[:, :, :], SS[:, :, :], float(gC),
                                                                             psB[:, :, :],
                                                                             op0=ALU.mult, op1=ALU.add),
                                 R=[("ps", 2), ("ps", 3), ("SS", 0), ("SS", 1)], W=[("SS", 0), ("SS", 1)])
                            S.op(ACT, lambda: nc.scalar.copy(SBF[:, :, :], SS[:, :, :]),
                                 R=[("SS", 0), ("SS", 1)], W=[("SBF", 0), ("SBF", 1)])
                            return bo

                        def stageB(cl, bo):
                            cs_ = slice(cl * CH, (cl + 1) * CH)
                            S.op(DVE, lambda: nc.vector.bn_stats(BST[:, 0:6], psum[bo][:, :]), R=[("ps", bo)], W=["BST"])
                            S.op(DVE, lambda: nc.vector.bn_aggr(BMV[:, 0:2], BST[:, 0:6]), R=["BST"], W=["BMV"])
                            S.op(ACT, lambda: nc.scalar.activation(BMV[:, 2:3], BMV[:, 1:2], AF.Sqrt, bias=vcol(("epsn", h))),
                                 R=["BMV", "VEC"], W=["BMV"])
                            S.op(DVE, lambda: nc.vector.reciprocal(BMV[:, 2:3], BMV[:, 2:3]), R=["BMV"], W=["BMV"])
                            S.op(DVE, lambda: nc.vector.tensor_scalar(ON[:, :], psum[bo][:, :], BMV[:, 0:1], BMV[:, 2:3],
                                                                      op0=ALU.subtract, op1=ALU.mult),
                                 R=[("ps", bo), "BMV"], W=["ON"])
                            for ech in range(4):
                                S.op(PE, lambda: nc.tensor.transpose(psb[:, ech * P:(ech + 1) * P],
                                                                     ON[:, ech * P:(ech + 1) * P], IDENT[:, :]),
                                     R=["ON", "IDENT"], W=["psb"], inc=(ech == 3))
                            for ech in range(4):
                                S.op(DVE, lambda: nc.vector.scalar_tensor_tensor(
                                    YT[:, ech, cs_], psb[:, ech * P:(ech + 1) * P], vcol(("gng", j), h * 4 + ech),
                                    SGT[:, ech, cs_], op0=ALU.mult, op1=ALU.mult),
                                    R=["psb", "VEC", ("SGT", sgpar, ech)], W=[("YT", ech)])

                        nxt = t + 1 < NT
                        bos = {0: stageA(0)}
                        if nxt:
                            Pg(t + 1, (0, 1))
                        bos[1] = stageA(1)
                        if nxt:
                            Pg(t + 1, (2, 3))
                        stageB(0, bos[0])
                        bos[2] = stageA(2)
                        stageB(1, bos[1])
                        bos[3] = stageA(3)
                        load_kv(h * NT + t + 1)
                        if nxt:
                            Pq(t + 1)
                        stageB(2, bos[2])
                        stageB(3, bos[3])
                        for oc in range(KC):
                            bo2 = bank("C")
                            for ech in range(4):
                                mm(psum[bo2][:, :], O[:, ech, oc * P:(oc + 1) * P], YT[:, ech, :], ech == 0, ech == 3,
                                   R=[("W", so), ("YT", ech)], W=[("ps", bo2)])
                            residual_add(oc, t, bo2)
                S.barrier()

        XT = sb("XT", [P, KC, HALO], F32)
        first = True
        for li in cfg.layers:
            if li % 2 == 0:
                if not first:
                    exchange_halo(XT)
                conv_module(li // 2, li)
            else:
                retention(li // 2, li)
            exchange_halo(XT)
            ffn(li)
            first = False

        if cfg.final_norm:
            with contextlib.ExitStack() as es:
                SQ = [es.enter_context(nc.sbuf_tensor(uname(f"n_sq{k}"), [P, TW], BF16)) for k in range(2)]
                RS = es.enter_context(nc.sbuf_tensor("n_rs", [P, TW], F32))
                for t in range(NT):
                    b = bank("D")
                    for c in range(KC):
                        q = SQ[c % 2]
                        S.op(ACT, lambda: nc.scalar.activation(q[:, :], X[:, c, xc(t)], AF.Square),
                             R=[("X", c, t)], W=[("SQ", c % 2)])
                        mm(psum[b][:, :], ONES[:, :], q[:, :], c == 0, c == KC - 1,
                           R=[("SQ", c % 2), "ONES"], W=[("ps", b)], inc=True)
                    S.op(ACT, lambda: nc.scalar.activation(RS[:, :], psum[b][:, :], AF.Ln, bias=EPSV[:, 0:1],
                                                           scale=1.0 / D), R=[("ps", b), "EPSV"], W=["RS"])
                    S.op(ACT, lambda: nc.scalar.activation(RS[:, :], RS[:, :], AF.Exp, scale=-0.5), R=["RS"], W=["RS"])
                    for c in range(KC):
                        S.op(DVE, lambda: nc.vector.scalar_tensor_tensor(
                            X[:, c, xc(t)], X[:, c, xc(t)], vcol("nfin", c), RS[:, :], op0=ALU.mult, op1=ALU.mult),
                            R=[("X", c, t), "RS", "VEC"], W=[("X", c, t)])
                    if cfg.final_norm:
                        S.dma(SP, st_out, outT3[:, :, t * TW:(t + 1) * TW], X[:, :, xc(t)],
                              R=[("X", c, t) for c in range(KC)], W=[("out", t)])
                S.barrier()
        if not cfg.final_norm:
            for t in range(NT):
                S.dma(SP, st_out, outT3[:, :, t * TW:(t + 1) * TW], X[:, :, xc(t)],
                      R=[("X", c, t) for c in range(KC)], W=[("out", t)])
        S.finish()
        assert ws.next == len(ws.plan), (ws.next, len(ws.plan))
    return nc


def pack_vecs(inp, T, odd):
    voff, NV = vec_layout()
    v = np.zeros((P, NV), np.float32)

    def put(name, arr, nchunk):
        v[:, voff[name]:voff[name] + nchunk] = np.asarray(arr, np.float32).reshape(nchunk, P).T
    for i in range(4):
        put(("nmix", i), inp["norm_mix_g"][i], KC)
        put(("nffn", i), inp["norm_ffn_g"][i], KC)
        for tap in range(3):
            v[:, voff[("fdw", i)] + tap * NFC: voff[("fdw", i)] + (tap + 1) * NFC] = \
                np.asarray(inp["ffn_dw_w"][i][tap], np.float32).reshape(NFC, P).T
        put(("fdb", i), inp["ffn_dw_b"][i], NFC)
    put("nfin", inp["final_g"], KC)
    for j in range(2):
        for tap in range(CW):
            v[:, voff[("cdw", j)] + tap * KC: voff[("cdw", j)] + (tap + 1) * KC] = \
                np.asarray(inp["conv_dw_w"][j][tap], np.float32).reshape(KC, P).T
        put(("cdb", j), inp["conv_dw_b"][j], KC)
        put(("clg", j), inp["conv_ln_g"][j], KC)
        put(("clb", j), inp["conv_ln_b"][j], KC)
        put(("gng", j), inp["ret_gn_g"][j], 16)
    v[:, voff["flag"]] = 1.0 if odd else 0.0
    jj = np.arange(P, dtype=np.float32)
    v[:, voff["invf"]] = (1.0 / (np.float32(10000.0) ** (np.arange(0, DK, 2, dtype=np.float32) / np.float32(DK)))).astype(np.float32)
    scale = DK ** -0.5
    m = np.arange(CH, dtype=np.float64)
    for h in range(NH):
        g = 1.0 - 2.0 ** (-5.0 - h)
        v[:, voff[("kdec", h)]] = (scale * g ** (CH - 1 - m)).astype(np.float32)
        v[:, voff[("qc", h)]] = (g ** (m + 1.0)).astype(np.float32)
        v[:, voff[("qc2", h)]] = (g ** (2.0 * (m + 1.0))).astype(np.float32)
        v[:, voff[("epsn", h)]] = (EPS / g ** (2.0 * (m + 1.0))).astype(np.float32)
        for c in range(T // CH):
            v[:, voff[("gd", h)] + c] = (scale * g ** (T - 1 - (c * CH + m))).astype(np.float32)
    return v


def make_consts():
    cm = np.zeros((P, NH * 2 * CH), np.float32)
    m = np.arange(CH, dtype=np.float64)[:, None]
    n = np.arange(CH, dtype=np.float64)[None, :]
    for h in range(NH):
        g = 1.0 - 2.0 ** (-5.0 - h)
        scale = DK ** -0.5
        cm[:, (2 * h) * CH:(2 * h + 1) * CH] = np.where(m <= n, scale * g ** (-(m + 1.0)), 0.0)
        cm[:, (2 * h + 1) * CH:(2 * h + 2) * CH] = np.broadcast_to(g ** (n + 1.0), (P, CH))
    return cm


_PROG_CACHE = {}


def run_layers(inputs, layers, final_norm, x_override=None):
    x = np.asarray(inputs["x"], np.float32) if x_override is None else x_override
    B, SEQ, _ = x.shape
    T = SEQ // 2
    key = (T, tuple(layers), final_norm)
    if key not in _PROG_CACHE:
        _PROG_CACHE[key] = build_program(Cfg(T, list(layers), final_norm))
    nc = _PROG_CACHE[key]
    posi = np.asarray(inputs["positions"], np.int32)
    shared = {k: np.ascontiguousarray(np.asarray(inputs[k], np.float32)) for k in
              ("conv_w_in", "conv_w_out", "ret_w_in", "ret_w_out", "ffn_w_in", "ffn_w_out")}
    cm = make_consts()
    ident = np.eye(P, dtype=np.float32)
    in_maps = []
    for core in range(8):
        b, s = core // 2, core % 2
        xT = np.zeros((D, HALO + T), np.float32)
        xT[:, HALO:] = x[b, s * T:(s + 1) * T, :].T
        if s == 1:
            xT[:, :HALO] = x[b, T - HALO:T, :].T
        m = dict(shared)
        m["xT"] = xT
        m["pos"] = np.ascontiguousarray(posi[b, s * T:(s + 1) * T].reshape(1, T))
        m["vecs"] = pack_vecs(inputs, T, s == 1)
        m["cmask"] = cm
        m["ident"] = ident
        in_maps.append(m)
    res = run_bass_kernel_spmd(nc, in_maps, core_ids=list(range(8)))
    out = np.zeros((B, SEQ, D), np.float32)
    for core in range(8):
        b, s = core // 2, core % 2
        out[b, s * T:(s + 1) * T, :] = res.results[core]["outT"].T
    return out


def kernel(**inputs):
    return run_layers(inputs, [0, 1, 2, 3], True)
```
